# Optimizing a Trainium2 kernel written in Bass

```python
import math
import jax, jax.numpy as jnp
from jax import lax
import numpy as np

D_MODEL = 1024
BATCH = 32
SEQ = 2048
DEPTH = 2

HEAD_DIM = 64
BLOCK = 128
LRU_WIDTH = D_MODEL // 2
LRU_BLOCKS = 8
LRU_BLOCK = LRU_WIDTH // LRU_BLOCKS
CONV_WIDTH = 4
LRU_C = 8.0
FOX_HEADS = (D_MODEL // 2) // HEAD_DIM
DIL_HEADS = (D_MODEL // 2) // HEAD_DIM
DIL_PATTERN = ((128, 1), (512, 4), (2048, 16))
SWA_Q_HEADS = (D_MODEL // 2) // HEAD_DIM
SWA_KV_HEADS = 2
SWA_WINDOW = 128
MEM_LEN = 256
XA_HEADS = 4
XA_HEAD_DIM = D_MODEL // XA_HEADS
D_FF = 4 * D_MODEL
ROPE_THETA = 10000.0
EPS = 1e-6

N_EVEN = (DEPTH + 1) // 2
N_ODD = DEPTH // 2
FOX_W = FOX_HEADS * HEAD_DIM
DIL_W = DIL_HEADS * HEAD_DIM
SWA_QW = SWA_Q_HEADS * HEAD_DIM
SWA_KW = SWA_KV_HEADS * HEAD_DIM
AB_IN = 2 * LRU_WIDTH + 3 * FOX_W + FOX_HEADS
AB_MIX = LRU_WIDTH + FOX_W
CD_IN = 3 * DIL_W + SWA_QW + 2 * SWA_KW
CD_MIX = DIL_W + SWA_QW

kernel_name = "hybrid_rglru_fox_dilated_swa_trunk"


def rmsnorm(x, g):
    xf = x.astype(jnp.float32)
    y = xf * lax.rsqrt(jnp.mean(xf * xf, axis=-1, keepdims=True) + EPS)
    return (y * g.astype(jnp.float32)).astype(x.dtype)


def rope(x, pos):
    half = x.shape[-1] // 2
    inv = ROPE_THETA ** (-jnp.arange(half, dtype=jnp.float32) / half)
    ang = pos.astype(jnp.float32)[:, None] * inv[None, :]
    cos = jnp.cos(ang)[None, :, None, :]
    sin = jnp.sin(ang)[None, :, None, :]
    xf = x.astype(jnp.float32)
    x1, x2 = xf[..., :half], xf[..., half:]
    return jnp.concatenate([x1 * cos - x2 * sin, x2 * cos + x1 * sin], axis=-1).astype(x.dtype)


def causal_depthwise_conv(u, w, b):
    out = lax.conv_general_dilated(
        u, w[:, None, :].astype(u.dtype), window_strides=(1,),
        padding=[(CONV_WIDTH - 1, 0)], dimension_numbers=("NWC", "WIO", "NWC"),
        feature_group_count=u.shape[-1])
    return out + b.astype(u.dtype)


def _linear_recurrence_combine(e1, e2):
    a1, b1 = e1
    a2, b2 = e2
    return a1 * a2, a2 * b1 + b2


def rg_lru(u, w_a, b_a, w_i, b_i, lam):
    bsz, s, c = u.shape
    uf = u.astype(jnp.float32)
    ub = uf.reshape(bsz, s, LRU_BLOCKS, LRU_BLOCK)
    r = jax.nn.sigmoid(jnp.einsum("bsgc,gcd->bsgd", ub, w_a.astype(jnp.float32)).reshape(bsz, s, c) + b_a.astype(jnp.float32))
    i = jax.nn.sigmoid(jnp.einsum("bsgc,gcd->bsgd", ub, w_i.astype(jnp.float32)).reshape(bsz, s, c) + b_i.astype(jnp.float32))
    log_a = -LRU_C * r * jax.nn.softplus(-lam.astype(jnp.float32))
    a = jnp.exp(log_a)
    x_in = jnp.sqrt(-jnp.expm1(2.0 * log_a)) * (i * uf)
    _, h = lax.associative_scan(_linear_recurrence_combine, (a, x_in), axis=1)
    return h.astype(u.dtype)


def forgetting_attention(q, k, v, log_f):
    s_len = q.shape[1]
    scale = q.shape[-1] ** -0.5
    c = jnp.cumsum(log_f, axis=1).transpose(0, 2, 1)
    outs = []
    for blk in range(s_len // BLOCK):
        q0, q1 = blk * BLOCK, (blk + 1) * BLOCK
        sc = jnp.einsum("bqhd,bkhd->bhqk", q[:, q0:q1], k[:, :q1]).astype(jnp.float32) * scale
        bias = c[:, :, q0:q1, None] - c[:, :, None, :q1]
        mask = jnp.arange(q0, q1)[:, None] >= jnp.arange(q1)[None, :]
        p = jax.nn.softmax(jnp.where(mask, sc + bias, -jnp.inf), axis=-1)
        outs.append(jnp.einsum("bhqk,bkhd->bqhd", p, v[:, :q1].astype(jnp.float32)))
    return jnp.concatenate(outs, axis=1).astype(q.dtype)


def banded_partial(q, k, v, max_dist):
    n, l, hk, g, dh = q.shape
    nb = -(-l // BLOCK)
    lp = nb * BLOCK
    pad = lp - l
    qb = jnp.pad(q, ((0, 0), (0, pad), (0, 0), (0, 0), (0, 0))).reshape(n, nb, BLOCK, hk, g, dh)
    kp = jnp.pad(k, ((0, 0), (BLOCK, pad), (0, 0), (0, 0))).reshape(n, nb + 1, BLOCK, hk, dh)
    vp = jnp.pad(v, ((0, 0), (BLOCK, pad), (0, 0), (0, 0))).reshape(n, nb + 1, BLOCK, hk, dh)
    kb = jnp.concatenate([kp[:, :-1], kp[:, 1:]], axis=2)
    vb = jnp.concatenate([vp[:, :-1], vp[:, 1:]], axis=2)
    sc = jnp.einsum("nbqhgd,nbkhd->nbhgqk", qb, kb).astype(jnp.float32) * dh ** -0.5
    qpos = jnp.arange(BLOCK)[:, None] + BLOCK
    kpos = jnp.arange(2 * BLOCK)[None, :]
    dist = qpos - kpos
    blk = jnp.arange(nb)[:, None, None]
    valid = (dist >= 0) & (dist <= max_dist) & (blk * BLOCK - BLOCK + kpos >= 0)
    sc = jnp.where(valid[None, :, None, None], sc, -jnp.inf)
    m = jnp.max(sc, axis=-1)
    p = jnp.exp(sc - m[..., None])
    den = jnp.sum(p, axis=-1)
    num = jnp.einsum("nbhgqk,nbkhd->nbqhgd", p, vb.astype(jnp.float32)).reshape(n, lp, hk, g, dh)[:, :l]
    m = m.transpose(0, 1, 4, 2, 3).reshape(n, lp, hk, g)[:, :l]
    den = den.transpose(0, 1, 4, 2, 3).reshape(n, lp, hk, g)[:, :l]
    return num, m, den


def dilated_attention(q, k, v):
    bsz, s, h, dh = q.shape
    nums, ms, dens = [], [], []
    for window, dil in DIL_PATTERN:
        l = s // dil

        def to_classes(t):
            return t.reshape(bsz, l, dil, h, dh).transpose(0, 2, 1, 3, 4).reshape(bsz * dil, l, h, dh)

        num, m, den = banded_partial(to_classes(q)[:, :, :, None, :], to_classes(k), to_classes(v), window // dil)
        nums.append(num[:, :, :, 0].reshape(bsz, dil, l, h, dh).transpose(0, 2, 1, 3, 4).reshape(bsz, s, h, dh))
        ms.append(m[..., 0].reshape(bsz, dil, l, h).transpose(0, 2, 1, 3).reshape(bsz, s, h))
        dens.append(den[..., 0].reshape(bsz, dil, l, h).transpose(0, 2, 1, 3).reshape(bsz, s, h))
    m_tot = ms[0]
    for m in ms[1:]:
        m_tot = jnp.maximum(m_tot, m)
    num_tot = sum(nm * jnp.exp(m - m_tot)[..., None] for nm, m in zip(nums, ms))
    den_tot = sum(dn * jnp.exp(m - m_tot) for dn, m in zip(dens, ms))
    return (num_tot / den_tot[..., None]).astype(q.dtype)


def sliding_window_sink_attention(q, k, v, sink):
    bsz, s, hq, dh = q.shape
    hk = k.shape[2]
    g = hq // hk
    num, m, den = banded_partial(q.reshape(bsz, s, hk, g, dh), k, v, SWA_WINDOW - 1)
    sink_g = sink.astype(jnp.float32).reshape(hk, g)
    m2 = jnp.maximum(m, sink_g)
    w = jnp.exp(m - m2)
    out = num * w[..., None] / (den * w + jnp.exp(sink_g - m2))[..., None]
    return out.reshape(bsz, s, hq, dh).astype(q.dtype)


def mixer_rglru_fox(h, w_in, conv_w, conv_b, w_a, b_a, w_i, b_i, lam, b_f, w_out):
    bsz, s, _ = h.shape
    z = h @ w_in
    u, gate, q, k, v, f_logit = jnp.split(
        z, [LRU_WIDTH, 2 * LRU_WIDTH, 2 * LRU_WIDTH + FOX_W, 2 * LRU_WIDTH + 2 * FOX_W, 2 * LRU_WIDTH + 3 * FOX_W], axis=-1)
    y_a = rg_lru(causal_depthwise_conv(u, conv_w, conv_b), w_a, b_a, w_i, b_i, lam) * jax.nn.gelu(gate)
    log_f = jax.nn.log_sigmoid(f_logit.astype(jnp.float32) + b_f.astype(jnp.float32))
    shp = (bsz, s, FOX_HEADS, HEAD_DIM)
    y_b = forgetting_attention(q.reshape(shp), k.reshape(shp), v.reshape(shp), log_f).reshape(bsz, s, FOX_W)
    return jnp.concatenate([y_a, y_b], axis=-1) @ w_out


def mixer_dilated_swa(h, w_in, sink, w_out):
    bsz, s, _ = h.shape
    pos = jnp.arange(s)
    z = h @ w_in
    qc, kc, vc, qd, kd, vd = jnp.split(
        z, [DIL_W, 2 * DIL_W, 3 * DIL_W, 3 * DIL_W + SWA_QW, 3 * DIL_W + SWA_QW + SWA_KW], axis=-1)
    dshp = (bsz, s, DIL_HEADS, HEAD_DIM)
    y_c = dilated_attention(rope(qc.reshape(dshp), pos), rope(kc.reshape(dshp), pos), vc.reshape(dshp))
    kshp = (bsz, s, SWA_KV_HEADS, HEAD_DIM)
    y_d = sliding_window_sink_attention(rope(qd.reshape(bsz, s, SWA_Q_HEADS, HEAD_DIM), pos),
                                        rope(kd.reshape(kshp), pos), vd.reshape(kshp), sink)
    return jnp.concatenate([y_c.reshape(bsz, s, DIL_W), y_d.reshape(bsz, s, SWA_QW)], axis=-1) @ w_out


def memory_cross_attention(h, mem, g_mem, w_q, w_kv, w_o):
    bsz, s, _ = h.shape
    mem_n = rmsnorm(mem, g_mem)
    q = (h @ w_q).reshape(bsz, s, XA_HEADS, XA_HEAD_DIM)
    k, v = jnp.split(mem_n @ w_kv, 2, axis=-1)
    k = k.reshape(bsz, -1, XA_HEADS, XA_HEAD_DIM)
    v = v.reshape(bsz, -1, XA_HEADS, XA_HEAD_DIM)
    sc = jnp.einsum("bqhd,bkhd->bhqk", q, k).astype(jnp.float32) * XA_HEAD_DIM ** -0.5
    p = jax.nn.softmax(sc, axis=-1)
    o = jnp.einsum("bhqk,bkhd->bqhd", p, v.astype(jnp.float32)).astype(h.dtype)
    return o.reshape(bsz, s, D_MODEL) @ w_o


def squared_relu_mlp(h, w_up, w_down):
    return jnp.square(jax.nn.relu(h @ w_up)) @ w_down


def setup_inputs(seed: int = 0) -> dict:
    key = jax.random.key(seed)
    ks = iter(jax.random.split(key, 40))
    f32 = jnp.float32

    def nrm(shape, scale):
        return scale * jax.random.normal(next(ks), shape, f32)

    def gain(shape):
        return 1.0 + 0.05 * jax.random.normal(next(ks), shape, f32)

    a8 = jax.random.uniform(next(ks), (N_EVEN, LRU_WIDTH), f32, 0.9, 0.999)
    a0 = a8 ** (1.0 / LRU_C)
    lru_lambda = jnp.log(a0) - jnp.log1p(-a0)
    return {
        "x": nrm((BATCH, SEQ, D_MODEL), 1.0),
        "mem": nrm((BATCH, MEM_LEN, D_MODEL), 1.0),
        "ab_norm": gain((N_EVEN, D_MODEL)),
        "ab_w_in": nrm((N_EVEN, D_MODEL, AB_IN), D_MODEL ** -0.5),
        "ab_conv_w": nrm((N_EVEN, CONV_WIDTH, LRU_WIDTH), CONV_WIDTH ** -0.5),
        "ab_conv_b": nrm((N_EVEN, LRU_WIDTH), 0.05),
        "lru_w_a": nrm((N_EVEN, LRU_BLOCKS, LRU_BLOCK, LRU_BLOCK), LRU_BLOCK ** -0.5),
        "lru_b_a": nrm((N_EVEN, LRU_WIDTH), 0.05),
        "lru_w_i": nrm((N_EVEN, LRU_BLOCKS, LRU_BLOCK, LRU_BLOCK), LRU_BLOCK ** -0.5),
        "lru_b_i": nrm((N_EVEN, LRU_WIDTH), 0.05),
        "lru_lambda": lru_lambda,
        "fox_b_f": jax.random.uniform(next(ks), (N_EVEN, FOX_HEADS), f32, 1.0, 4.0),
        "ab_w_out": nrm((N_EVEN, AB_MIX, D_MODEL), AB_MIX ** -0.5),
        "cd_norm": gain((N_ODD, D_MODEL)),
        "cd_w_in": nrm((N_ODD, D_MODEL, CD_IN), D_MODEL ** -0.5),
        "cd_sink": nrm((N_ODD, SWA_Q_HEADS), 0.5),
        "cd_w_out": nrm((N_ODD, CD_MIX, D_MODEL), CD_MIX ** -0.5),
        "xa_norm": gain((DEPTH, D_MODEL)),
        "xa_mem_norm": gain((DEPTH, D_MODEL)),
        "xa_w_q": nrm((DEPTH, D_MODEL, D_MODEL), D_MODEL ** -0.5),
        "xa_w_kv": nrm((DEPTH, D_MODEL, 2 * D_MODEL), D_MODEL ** -0.5),
        "xa_w_o": nrm((DEPTH, D_MODEL, D_MODEL), D_MODEL ** -0.5),
        "mlp_norm": gain((DEPTH, D_MODEL)),
        "mlp_w_up": nrm((DEPTH, D_MODEL, D_FF), D_MODEL ** -0.5),
        "mlp_w_down": nrm((DEPTH, D_FF, D_MODEL), D_FF ** -0.5),
        "final_norm": gain((D_MODEL,)),
    }


def reference(x, mem, ab_norm, ab_w_in, ab_conv_w, ab_conv_b, lru_w_a, lru_b_a, lru_w_i, lru_b_i,
              lru_lambda, fox_b_f, ab_w_out, cd_norm, cd_w_in, cd_sink, cd_w_out,
              xa_norm, xa_mem_norm, xa_w_q, xa_w_kv, xa_w_o, mlp_norm, mlp_w_up, mlp_w_down, final_norm):
    h = x
    for layer in range(DEPTH):
        j = layer // 2
        if layer % 2 == 0:
            h = h + mixer_rglru_fox(rmsnorm(h, ab_norm[j]), ab_w_in[j], ab_conv_w[j], ab_conv_b[j],
                                    lru_w_a[j], lru_b_a[j], lru_w_i[j], lru_b_i[j], lru_lambda[j],
                                    fox_b_f[j], ab_w_out[j])
        else:
            h = h + mixer_dilated_swa(rmsnorm(h, cd_norm[j]), cd_w_in[j], cd_sink[j], cd_w_out[j])
        h = h + memory_cross_attention(rmsnorm(h, xa_norm[layer]), mem, xa_mem_norm[layer],
                                       xa_w_q[layer], xa_w_kv[layer], xa_w_o[layer])
        h = h + squared_relu_mlp(rmsnorm(h, mlp_norm[layer]), mlp_w_up[layer], mlp_w_down[layer])
    return rmsnorm(h, final_norm)
```

```python
import os
import numpy as np
import concourse.bass as bass
import concourse.mybir as mybir
from concourse.bass_utils import run_bass_kernel_spmd

F32 = mybir.dt.float32
BF16 = mybir.dt.bfloat16
AF = mybir.ActivationFunctionType
ALU = mybir.AluOpType
AX = mybir.AxisListType

ENG_NAMES = ("pe", "act", "dve", "pool", "sp")
SEM_ROT = 2000
DMA_ROT = 120


class Buf:
    __slots__ = ("name", "w", "r", "rd", "const")

    def __init__(self, name):
        self.name = name
        self.w = None
        self.r = {}
        self.rd = []
        self.const = False


class Op:
    __slots__ = ("eng", "fn", "deps", "is_dma", "dkey", "sig", "dval", "id")


class Prog:
    def __init__(self, nc):
        self.nc = nc
        self.ops = []
        self.streams = {e: [] for e in ENG_NAMES}
        self.dma_keys = {}
        self.pending_bar = {e: None for e in ENG_NAMES}
        self.dma_since_bar = []

    def barrier(self):
        deps = set(self.dma_since_bar)
        for e in ENG_NAMES:
            if self.streams[e]:
                deps.add(self.streams[e][-1].id)
            if self.pending_bar[e]:
                deps |= self.pending_bar[e]
        for e in ENG_NAMES:
            self.pending_bar[e] = set(deps)
        self.dma_since_bar = []

    def op(self, eng, fn, R=(), W=(), dma=None):
        o = Op()
        o.eng = eng
        o.fn = fn
        o.is_dma = dma is not None
        o.dkey = dma
        o.sig = None
        o.dval = None
        o.id = len(self.ops)
        deps = set()
        raw = set()
        for b in R:
            if b.w is not None:
                deps.add(b.w)
                raw.add(b.w)
        for b in W:
            if b.w is not None:
                deps.add(b.w)
            deps.update(b.r.values())
            deps.update(b.rd)
        keep = set()
        for d in deps:
            od = self.ops[d]
            if od.eng == eng and not od.is_dma and not o.is_dma:
                if eng != "pe":
                    keep.add(d)
            else:
                keep.add(d)
        if self.pending_bar[eng] is not None:
            for d in self.pending_bar[eng]:
                od = self.ops[d]
                if od.is_dma or od.eng != eng:
                    keep.add(d)
            self.pending_bar[eng] = None
        o.deps = keep
        for b in R:
            if b.const:
                continue
            if o.is_dma:
                b.rd.append(o.id)
            else:
                b.r[eng] = o.id
        for b in W:
            b.w = o.id
            b.r = {}
            b.rd = []
        self.ops.append(o)
        self.streams[eng].append(o)
        if o.is_dma:
            self.dma_keys.setdefault(dma, 0)
            self.dma_since_bar.append(o.id)
        return o

    def emit(self, final_wait_ops=()):
        nc = self.nc
        ops = self.ops
        needed = set()
        for o in ops:
            for d in o.deps:
                needed.add(d)
        for o in final_wait_ops:
            needed.add(o.id)
        ncnt = {e: 0 for e in ENG_NAMES}
        dcnt = {k: 0 for k in self.dma_keys}
        for o in ops:
            if o.is_dma:
                dcnt[o.dkey] += 1
                o.dval = dcnt[o.dkey]
            elif o.id in needed:
                ncnt[o.eng] += 1
                o.sig = ncnt[o.eng]
        from contextlib import ExitStack
        with ExitStack() as es:
            esems = {}
            for e in ENG_NAMES:
                n = max(1, (ncnt[e] + SEM_ROT - 1) // SEM_ROT)
                esems[e] = [es.enter_context(nc.semaphore(f"s_{e}{i}")) for i in range(n)]
            dsems = {k: [es.enter_context(nc.semaphore(f"d_{k}_{i}")) for i in range((dcnt[k] + DMA_ROT - 1) // DMA_ROT)]
                     for k in self.dma_keys}
            self.nsems = sum(len(v) for v in esems.values()) + sum(len(v) for v in dsems.values())
            block = es.enter_context(nc.Block())

            def sem_of(o):
                if o.is_dma:
                    n = o.dval - 1
                    return dsems[o.dkey][n // DMA_ROT], 16 * ((n % DMA_ROT) + 1)
                s = o.sig - 1
                return esems[o.eng][s // SEM_ROT], (s % SEM_ROT) + 1

            def run(ename, engobj):
                waited = {}
                for o in self.streams[ename]:
                    need = {}
                    for d in o.deps:
                        od = ops[d]
                        if od.is_dma:
                            key = ("d", od.dkey)
                            v = od.dval
                        else:
                            key = ("e", od.eng)
                            v = od.sig
                        if waited.get(key, 0) >= v:
                            continue
                        if need.get(key, (0, None))[0] < v:
                            need[key] = (v, od)
                    for key, (v, od) in need.items():
                        s, sv = sem_of(od)
                        engobj.wait_ge(s, sv)
                        waited[key] = v
                    ins = o.fn(engobj)
                    if o.is_dma:
                        ins.then_inc(sem_of(o)[0], 16)
                    elif o.sig is not None:
                        s, sv = sem_of(o)
                        ins.then_inc(s, 1)
                if ename == "sp":
                    last = {}
                    for o in final_wait_ops:
                        last[o.dkey] = o
                    for o in last.values():
                        s, sv = sem_of(o)
                        engobj.wait_ge(s, sv)

            @block.tensor
            def _(e):
                run("pe", e)

            @block.scalar
            def _(e):
                run("act", e)

            @block.vector
            def _(e):
                run("dve", e)

            @block.gpsimd
            def _(e):
                run("pool", e)

            @block.sync
            def _(e):
                run("sp", e)
        return ncnt


S = 2048
D = 1024
NT = 16
NCORES = 8
SEQ_PER_CORE = 4
AB_IN = 2568
CD_IN = 2304
ARENA_WORDS = 51200
OFF_H = 0
OFF_XT = 65536
OFF_CONST = 98304
OFF_SCR = 109568
NCST = 128 * 4 + 18 * 128
NPAR = 232
M_CAUSAL = 0
M_SWA1 = 17


def _host_consts():
    ident = np.eye(128, dtype=np.float32)
    s = np.arange(128)[:, None]
    t = np.arange(128)[None, :]
    tri = (s <= t).astype(np.float32)
    ones = np.ones((128, 128), np.float32)
    pm = np.zeros((128, 128), np.float32)
    for hh in range(2):
        for i in range(32):
            pm[64 * hh + i + 32, 64 * hh + i] = -1.0
            pm[64 * hh + i, 64 * hh + i + 32] = 1.0
    masks = np.zeros((128, 18, 128), np.float32)
    for dlt in range(16):
        dist = 128 * dlt + (t - s)
        m = ((dist >= 0) & (dist <= 128)).astype(np.float32)
        m += ((dist >= 0) & (dist <= 512) & (dist % 4 == 0)).astype(np.float32)
        m += ((dist >= 0) & (dist <= 2048) & (dist % 16 == 0)).astype(np.float32)
        masks[:, dlt, :] = m
    masks[:, 16, :] = (s <= t).astype(np.float32)
    masks[:, 17, :] = (s > t).astype(np.float32)
    cst = np.concatenate([ident, tri, ones, pm, masks.reshape(128, -1)], axis=1).astype(np.float32)
    half = 32
    inv = (10000.0 ** (-np.arange(half, dtype=np.float32) / half)).astype(np.float32)
    pos = np.arange(S, dtype=np.float32)
    ang = pos[None, :] * inv[:, None]
    cos = np.cos(ang).astype(np.float32)
    sin = np.sin(ang).astype(np.float32)
    rope = np.zeros((128, 2, S), np.float32)
    for r in range(128):
        rope[r, 0] = cos[r % 32]
        rope[r, 1] = sin[r % 32]
    return cst, rope


def _pack_params(inp):
    p = np.zeros((128, NPAR), np.float32)

    def col8(v):
        return np.ascontiguousarray(np.asarray(v, np.float32).reshape(8, 128).T)
    p[:, 0:8] = col8(inp["ab_norm"][0])
    p[:, 8:16] = col8(inp["cd_norm"][0])
    p[:, 16:24] = col8(inp["xa_norm"][0])
    p[:, 24:32] = col8(inp["xa_norm"][1])
    p[:, 32:40] = col8(inp["xa_mem_norm"][0])
    p[:, 40:48] = col8(inp["xa_mem_norm"][1])
    p[:, 48:56] = col8(inp["mlp_norm"][0])
    p[:, 56:64] = col8(inp["mlp_norm"][1])
    cw = np.asarray(inp["ab_conv_w"][0], np.float32)
    for c in range(4):
        for k in range(4):
            p[:, 64 + c * 4 + k] = cw[k, c * 128:(c + 1) * 128]
    p[:, 80:84] = np.asarray(inp["ab_conv_b"][0], np.float32).reshape(4, 128).T
    p[:, 84:88] = np.asarray(inp["lru_b_a"][0], np.float32).reshape(4, 128).T
    p[:, 88:92] = np.asarray(inp["lru_b_i"][0], np.float32).reshape(4, 128).T
    p[:, 92:96] = np.asarray(inp["lru_lambda"][0], np.float32).reshape(4, 128).T
    p[:, 96:104] = np.asarray(inp["cd_sink"][0], np.float32)[None, :]
    p[:, 104:232] = np.tile(np.asarray(inp["fox_b_f"][0], np.float32), 16)[None, :]
    return p


def build_program(nseq=SEQ_PER_CORE, stop=None, raw_out=False):
    from contextlib import ExitStack
    nc = bass.Bass("TRN2", target_bir_lowering=False)

    def din(name, shape, dt=F32):
        return nc.dram_tensor(name, list(shape), dt, kind="ExternalInput").ap()

    def dscr(name, shape, dt=BF16):
        return nc.dram_tensor(name, list(shape), dt, kind="Internal").ap()

    x_d = din("x", [nseq, S, D])
    mem_d = din("mem", [nseq, 256, D])
    w_ab_in = din("ab_w_in", [D, AB_IN])
    w_ab_out = din("ab_w_out", [D, D])
    w_cd_in = din("cd_w_in", [D, CD_IN])
    w_cd_out = din("cd_w_out", [D, D])
    w_q = din("xa_w_q", [2, D, D])
    w_kv = din("xa_w_kv", [2, D, 2 * D])
    w_o = din("xa_w_o", [2, D, D])
    w_up = din("mlp_w_up", [2, D, 4 * D])
    w_dn = din("mlp_w_down", [2, 4 * D, D])
    lru_wa = din("lru_w_a", [8, 64, 64])
    lru_wi = din("lru_w_i", [8, 64, 64])
    par_d = din("params", [128, NPAR])
    gfin_d = din("gfin", [128, D])
    cst_d = din("cst", [128, NCST])
    rope_d = din("rope", [128, 2, S])
    y_d = nc.dram_tensor("y", [nseq, S, D], F32, kind="ExternalOutput").ap()

    s_ab_in = dscr("s_ab_in", [D, AB_IN])
    s_ab_out = dscr("s_ab_out", [D, D])
    s_cd_in = dscr("s_cd_in", [D, CD_IN])
    s_cd_out = dscr("s_cd_out", [D, D])
    s_q = [dscr(f"s_q{l}", [D, D]) for l in range(2)]
    s_kv = [dscr(f"s_kv{l}", [D, 2 * D]) for l in range(2)]
    s_o = [dscr(f"s_o{l}", [D, D]) for l in range(2)]
    s_up = [dscr(f"s_up{l}", [D, 4 * D]) for l in range(2)]
    s_dn = [dscr(f"s_dn{l}", [4 * D, D]) for l in range(2)]

    P = Prog(nc)
    es = ExitStack()
    arena = es.enter_context(nc.sbuf_tensor("arena", [128, ARENA_WORDS], F32))
    pb = [es.enter_context(nc.psum_tensor(f"pb{i}", [128, 512], F32)) for i in range(8)]
    pq = [[Buf(f"pb{i}q{j}") for j in range(4)] for i in range(8)]
    bank_ctr = [0]

    def bank(lo=0, hi=8):
        i = lo + bank_ctr[0] % (hi - lo)
        bank_ctr[0] += 1
        return i

    def V(off, shape, dt):
        n = 1
        for s_ in shape[1:]:
            n *= s_
        nbytes = n * (4 if dt == F32 else 2)
        assert off % 4 == 0 and nbytes % 4 == 0, (off, shape)
        assert off + nbytes <= ARENA_WORDS * 4, (off, shape)
        ap = arena[:, off // 4:(off + nbytes) // 4]
        if dt != F32:
            ap = ap.bitcast(dt)
        if len(shape) == 3:
            ap = ap.rearrange("p (a b) -> p a b", a=shape[1])
        elif len(shape) == 4:
            ap = ap.rearrange("p (a b c) -> p a b c", a=shape[1], b=shape[2])
        if shape[0] < 128:
            ap = ap[0:shape[0]]
        return ap

    def mm(out, lhsT, rhs, st, sp, R, W):
        return P.op("pe", lambda e: e.matmul(out, lhsT=lhsT, rhs=rhs, start=st, stop=sp), R, W)

    def act(out, in_, func, R, W, bias=None, scale=None, accum=None):
        kw = {}
        if bias is not None:
            kw["bias"] = bias
        if scale is not None:
            kw["scale"] = scale
        if accum is not None:
            kw["accum_out"] = accum
        return P.op("act", lambda e: e.activation(out=out, in_=in_, func=func, **kw), R, W)

    def cp(eng, out, in_, R, W):
        if eng == "act":
            return P.op("act", lambda e: e.copy(out=out, in_=in_), R, W)
        return P.op(eng, lambda e: e.tensor_copy(out=out, in_=in_), R, W)

    def ts(eng, out, in0, s1, s2, op0, op1, R, W):
        if op1 is None:
            return P.op(eng, lambda e: e.tensor_scalar(out=out, in0=in0, scalar1=s1, scalar2=None, op0=op0), R, W)
        return P.op(eng, lambda e: e.tensor_scalar(out=out, in0=in0, scalar1=s1, scalar2=s2, op0=op0, op1=op1), R, W)

    def tt(eng, out, in0, in1, op, R, W):
        return P.op(eng, lambda e: e.tensor_tensor(out=out, in0=in0, in1=in1, op=op), R, W)

    def stt(out, in0, scalar, in1, op0, op1, R, W):
        return P.op("dve", lambda e: e.scalar_tensor_tensor(out=out, in0=in0, scalar=scalar, in1=in1, op0=op0, op1=op1), R, W)

    def dma(out, in_, R, W, key, eng="sp"):
        return P.op(eng, lambda e: e.dma_start(out=out, in_=in_), R, W, dma=key)

    def memset(eng, ap, val, W):
        return P.op(eng, lambda e: e.memset(ap, val), (), W)

    h = V(OFF_H, [128, NT, D], F32)
    bh = [Buf(f"h{i}") for i in range(NT)]
    xT = V(OFF_XT, [128, 8, S], BF16)
    bxT = [Buf(f"xT{i}") for i in range(NT)]
    c0 = OFF_CONST
    ident = V(c0, [128, 128], BF16); c0 += 256
    masks = V(c0, [128, 18, 128], BF16); c0 += 18 * 256
    tri = V(c0, [128, 128], F32); c0 += 512
    ones = V(c0, [128, 128], F32); c0 += 512
    pmat = V(c0, [128, 128], BF16); c0 += 256
    gfin = V(c0, [128, D], F32); c0 += 4096
    par = V(c0, [128, NPAR], F32); c0 += NPAR * 4
    nsp = V(c0, [128, 8], F32); c0 += 32
    sinkexp = V(c0, [128, 8], F32); c0 += 32
    cone = V(c0, [128, 2], F32); c0 += 8
    assert c0 <= OFF_SCR, c0
    b_cst = Buf("cst")
    one_ap = cone[:, 0:1]
    eps_ap = cone[:, 1:2]

    def setup():
        stg = V(0, [128, NCST], F32)
        b_stg = Buf("stg")
        dma(stg, cst_d, [], [b_stg], "init")
        dma(gfin, gfin_d, [], [b_cst], "init")
        dma(par, par_d, [], [b_cst], "init")
        P.barrier()
        cp("dve", ident, stg[:, 0:128], [b_stg], [b_cst])
        cp("dve", tri, stg[:, 128:256], [b_stg], [b_cst])
        cp("dve", ones, stg[:, 256:384], [b_stg], [b_cst])
        cp("dve", pmat, stg[:, 384:512], [b_stg], [b_cst])
        cp("dve", masks, stg[:, 512:NCST].rearrange("p (a b) -> p a b", a=18), [b_stg], [b_cst])
        memset("dve", cone[:, 0:1], 1.0, [b_cst])
        memset("dve", cone[:, 1:2], 1e-6, [b_cst])
        act(nsp[:, 0:4], par[:, 92:96], AF.Exp, [b_cst], [b_cst], scale=-1.0)
        act(nsp[:, 0:4], nsp[:, 0:4], AF.Ln, [b_cst], [b_cst], bias=one_ap, scale=1.0)
        ts("dve", nsp[:, 4:8], nsp[:, 0:4], -16.0, None, ALU.mult, None, [b_cst], [b_cst])
        ts("dve", nsp[:, 0:4], nsp[:, 0:4], -8.0, None, ALU.mult, None, [b_cst], [b_cst])
        act(sinkexp, par[:, 96:104], AF.Exp, [b_cst], [b_cst])
        P.barrier()
        b_cst.const = True

    def prepass():
        specs = [
            (w_ab_in, 0, s_ab_in), (w_ab_out, None, s_ab_out),
            (w_cd_in, 8, s_cd_in), (w_cd_out, None, s_cd_out),
        ]
        for l in range(2):
            specs += [(w_q[l], 16 + 8 * l, s_q[l]), (w_kv[l], 32 + 8 * l, s_kv[l]), (w_o[l], None, s_o[l]),
                      (w_up[l], 48 + 8 * l, s_up[l]), (w_dn[l], None, s_dn[l])]
        NS = 4
        st = [V(i * 8192, [128, 2048], F32) for i in range(NS)]
        ob = [V(NS * 8192 + i * 4096, [128, 2048], BF16) for i in range(NS)]
        b_st = [Buf(f"pst{i}") for i in range(NS)]
        b_ob = [Buf(f"pob{i}") for i in range(NS)]
        b_dst = Buf("wscr")
        k = 0
        for (src, gcol, dst) in specs:
            R_, C_ = src.shape
            for rt in range(R_ // 128):
                for cc in range(0, C_, 2048):
                    cw = min(2048, C_ - cc)
                    sl = k % NS
                    dma(st[sl][:, :cw], src[rt * 128:(rt + 1) * 128, cc:cc + cw], [], [b_st[sl]], f"pst{sl}")
                    eng = ("dve", "pool", "act")[k % 3]
                    if gcol is None:
                        cp(eng, ob[sl][:, :cw], st[sl][:, :cw], [b_st[sl]], [b_ob[sl]])
                    else:
                        g = par[:, gcol + rt:gcol + rt + 1]
                        if eng == "act":
                            act(ob[sl][:, :cw], st[sl][:, :cw], AF.Copy, [b_st[sl]], [b_ob[sl]], scale=g)
                        else:
                            ts(eng, ob[sl][:, :cw], st[sl][:, :cw], g, None, ALU.mult, None, [b_st[sl]], [b_ob[sl]])
                    dma(dst[rt * 128:(rt + 1) * 128, cc:cc + cw], ob[sl][:, :cw], [b_ob[sl]], [b_dst], f"pob{sl}")
                    k += 1
        P.barrier()

    def norm_to_T(src3, bsrc, ntiles, dstT, bdst):
        o = OFF_SCR
        SS = V(o, [128, 16], F32); o += 64
        STD = V(o, [128, 16], F32); o += 64
        RSTD = V(o, [128, 16], F32); o += 64
        JUNK = [V(o + i * 2048, [128, D], BF16) for i in range(2)]; o += 4096
        XS = [V(o + i * 2048, [128, D], BF16) for i in range(2)]
        bSTD, bRSTD = Buf("std"), Buf("rstd")
        bSS = [Buf(f"ss{i}") for i in range(16)]
        bJ = [Buf("junk0"), Buf("junk1")]
        bXS = [Buf("xs0"), Buf("xs1")]
        for i in range(ntiles):
            act(JUNK[i % 2], src3[:, i, :], AF.Square, [bsrc[i]], [bJ[i % 2], bSS[i]], accum=SS[:, i:i + 1])
        act(STD[:, :ntiles], SS[:, :ntiles], AF.Sqrt, bSS[:ntiles] + [b_cst], [bSTD], bias=eps_ap, scale=1.0 / D)
        P.op("dve", lambda e: e.reciprocal(out=RSTD[:, :ntiles], in_=STD[:, :ntiles]), [bSTD], [bRSTD])
        for i in range(ntiles):
            k = i % 2
            ts("dve", XS[k], src3[:, i, :], RSTD[:, i:i + 1], None, ALU.mult, None, [bsrc[i], bRSTD], [bXS[k]])
            bk = bank()
            pbh = pb[bk][:].bitcast(BF16)
            for c in range(8):
                P.op("pe", (lambda o_, i_: (lambda e: e.transpose(out=o_, in_=i_, identity=ident)))(
                    pbh[:, c * 128:(c + 1) * 128], XS[k][:, c * 128:(c + 1) * 128]), [bXS[k], b_cst], pq[bk])
            cp("act" if i % 2 == 0 else "dve", dstT[:, :, i * 128:(i + 1) * 128],
               pbh.rearrange("p (a b) -> p a b", a=8), pq[bk], [bdst[i]])

    def load_wblock(dst_ap, src_scr, c0_, ncols, bdst, key, nk=8):
        v = src_scr.rearrange("(c p) n -> p c n", p=128)
        return dma(dst_ap, v[:, 0:nk, c0_:c0_ + ncols], [], [bdst], key)

    def proj_T(dst_ap, bdst, wblk, bw, tg, evac_eng, rows=128):
        bk = bank()
        for kc in range(8):
            mm(pb[bk][0:rows, :], wblk[:, kc, 0:rows], xT[:, kc, tg * 512:(tg + 1) * 512], kc == 0, kc == 7,
               [bw] + bxT[tg * 4:tg * 4 + 4], pq[bk])
        cp(evac_eng, dst_ap, pb[bk][0:rows, :], pq[bk], [bdst])
        return bk

    def out_proj(srcT_list, bsrc_fn, w_scr, wkey):
        o = OFF_SCR + 32768
        WO = V(o, [128, 8, D], BF16)
        bWO = [Buf("wo0"), Buf("wo1")]
        wv = w_scr.rearrange("(c p) n -> p c n", p=128)
        for ch in range(2):
            dma(WO[:, :, ch * 512:(ch + 1) * 512], wv[:, :, ch * 512:(ch + 1) * 512], [], [bWO[ch]], f"{wkey}{ch}")
        for i in range(NT):
            for ch in range(2):
                bk = bank()
                for kc in range(8):
                    mm(pb[bk][:, :], srcT_list[kc][:, i * 128:(i + 1) * 128], WO[:, kc, ch * 512:(ch + 1) * 512],
                       kc == 0, kc == 7, [bWO[ch]] + bsrc_fn(kc, i), pq[bk])
                hs = h[:, i, ch * 512:(ch + 1) * 512]
                tt("dve", hs, pb[bk][:, :], hs, ALU.add, pq[bk] + [bh[i]], [bh[i]])

    def attn_pair(items, qk_fn, bias_fn, mask_fn, v_fn, fin_fn, PT, bPT, scale):
        LA = 3
        n = len(items)
        state = {}
        grp_bank = {}
        gcount = [0]

        def qk(idx):
            it = items[idx]
            bk = idx % 5
            sc = pb[bk][:, 0:128]
            lhsT, rhs, Rb = qk_fn(it)
            mm(sc, lhsT, rhs, True, True, Rb, pq[bk])
            ps = idx % len(PT)
            b_ap, Rbias = bias_fn(it)
            if b_ap is None:
                act(PT[ps], sc, AF.Exp, pq[bk], [bPT[ps]], scale=scale)
            else:
                act(PT[ps], sc, AF.Exp, pq[bk] + Rbias, [bPT[ps]], bias=b_ap, scale=scale)
            m_ap = mask_fn(it)
            if m_ap is not None:
                eng = "dve" if (idx % 2 == 0) else "pool"
                tt(eng, PT[ps], PT[ps], m_ap, ALU.mult, [bPT[ps], b_cst], [bPT[ps]])

        def pv(idx):
            it = items[idx]
            grp, first, last = it[0], it[1], it[2]
            if first:
                grp_bank[grp] = 5 + (gcount[0] % 2)
                gcount[0] += 1
            bk = grp_bank[grp]
            ps = idx % len(PT)
            v_ap, Rv = v_fn(it)
            mm(pb[bk][:, 0:65], PT[ps], v_ap, first, last, [bPT[ps]] + Rv, pq[bk])
            if last:
                fin_fn(it, pb[bk][:, 0:65], pq[bk])
                del grp_bank[grp]

        for idx in range(n + LA):
            if idx < n:
                qk(idx)
            if idx - LA >= 0:
                pv(idx - LA)

    def attn_wide(items, qk_fn, extra_fn, bias_fn, mask_fn, v_fn, first_kb_fn, fin_fn, PT, bPT, scale):
        LA = 2
        n = len(items)

        def qk(idx):
            it = items[idx]
            w = (it["j1"] - it["j0"]) * 128
            bk = idx % 3
            sc = pb[bk][:, 0:w]
            lhsT, rhs, Rb = qk_fn(it)
            ex = extra_fn(it) if extra_fn is not None else None
            mm(sc, lhsT, rhs, True, ex is None, Rb, pq[bk])
            if ex is not None:
                mm(sc, ex[0], ex[1], False, True, ex[2], pq[bk])
            ps = idx % len(PT)
            b_ap, Rbias = bias_fn(it)
            if b_ap is None:
                act(PT[ps][:, 0:w], sc, AF.Exp, pq[bk], [bPT[ps]], scale=scale)
            else:
                act(PT[ps][:, 0:w], sc, AF.Exp, pq[bk] + Rbias, [bPT[ps]], bias=b_ap, scale=scale)
            for (c0_, cw_, m_ap) in mask_fn(it):
                tt("dve", PT[ps][:, c0_:c0_ + cw_], PT[ps][:, c0_:c0_ + cw_], m_ap, ALU.mult, [bPT[ps], b_cst], [bPT[ps]])

        def pv(idx):
            it = items[idx]
            kb, hh = it["kb"], it["hh"]
            ps = idx % len(PT)
            v_ap, Rv = v_fn(it)
            for j in range(it["j0"], it["j1"]):
                bk = 3 + (j % 4)
                c = (j - it["j0"]) * 128
                first = kb == first_kb_fn(j)
                last = kb == j
                mm(pb[bk][:, 0:65], PT[ps][:, c:c + 128], v_ap, first, last, [bPT[ps]] + Rv, pq[bk])
                if last:
                    fin_fn(j, hh, pb[bk][:, 0:65], pq[bk])

        for idx in range(n + LA):
            if idx < n:
                qk(idx)
            if idx - LA >= 0:
                pv(idx - LA)

    def mixer_l0():
        DBG = int(os.environ.get("DBG_L0", "99"))
        o = OFF_SCR
        YA = V(o, [128, 4, S], BF16); o += 16384
        YB = V(o, [128, 4, S], BF16); o += 16384
        WB = [V(o + i * 2048, [128, 8, 128], BF16) for i in range(6)]; o += 6 * 2048
        QT = V(o, [128, S], BF16); o += 4096
        KT = V(o, [128, S], BF16); o += 4096
        VA = V(o, [128, NT, 2, 65], BF16); o += 4160
        PT = [V(o + i * 1024, [128, 512], BF16) for i in range(4)]; o += 4096
        YTOK = [V(o + i * 256, [128, 128], BF16) for i in range(4)]; o += 1024
        BIASH = [V(o + i * 256, [128, 4, 16], F32) for i in range(2)]; o += 2048
        ONESR = V(o, [128, 128], BF16); o += 256
        DIFF = V(o, [128, 32], F32); o += 128
        FB = V(o, [128, 128], F32); o += 512
        LL = V(o, [128, 128], F32); o += 512
        PRE = V(o, [128, 136], F32); o += 544
        CPOS = V(o, [128, 16, 8], F32); o += 512
        WF = V(o, [128, 8, 8], BF16); o += 128
        WBD = V(o, [128, 8, 128], BF16); o += 2048
        U = V(o, [128, 516], F32); o += 2064
        CV = V(o, [128, 512], F32); o += 2048
        CVB = V(o, [128, 512], BF16); o += 1024
        RR = V(o, [128, 512], F32); o += 2048
        II = V(o, [128, 512], F32); o += 2048
        S2 = V(o, [128, 512], F32); o += 2048
        GG = V(o, [128, 512], F32); o += 2048
        TT_ = V(o, [128, 512], F32); o += 2048
        HS = [V(o + i * 2048, [128, 512], F32) for i in range(2)]; o += 4096
        WST = V(o, [128, 8, 128], F32)
        DQ = V(o, [128, S], BF16); o += 4096
        OREC = V(o, [128, 4], F32); o += 16
        assert o <= ARENA_WORDS * 4, o
        bYA = [[Buf(f"ya{c}_{t}") for t in range(4)] for c in range(4)]
        bYBh = [[[Buf(f"yb{c}_{hf}_{t}") for t in range(NT)] for hf in range(2)] for c in range(4)]
        bWB = [Buf(f"wb{i}") for i in range(6)]
        bQT = [Buf(f"qt{t}") for t in range(4)]
        bKT = [Buf(f"kt{t}") for t in range(4)]
        bVA = [Buf(f"va{t}") for t in range(NT)]
        bVAone = Buf("vaone")
        bPT = [Buf(f"pt{i}") for i in range(4)]
        bYTOK = [Buf(f"ytok{i}") for i in range(4)]
        bBIAS = [Buf("biash0"), Buf("biash1")]
        bONESR, bDIFF = Buf("onesr"), Buf("diff")
        bFB, bLL, bPRE, bCPOS, bWF, bWBD, bWST = (Buf(n_) for n_ in ("fb", "ll", "pre", "cpos", "wf", "wbd", "wst"))
        bU, bCV, bCVB, bRR, bII, bS2, bGG, bTT = (Buf(n_) for n_ in ("u", "cv", "cvb", "rr", "ii", "s2", "gg", "tt"))
        bHS = [Buf("hs0"), Buf("hs1")]
        bOREC = Buf("orec")
        wsrc = s_ab_in
        wctr = [0]

        def wload(c0_, ncols=128):
            i = wctr[0] % 6
            wctr[0] += 1
            load_wblock(WB[i][:, :, 0:ncols], wsrc, c0_, ncols, bWB[i], f"wb{i}")
            return WB[i], bWB[i]

        memset("pool", WST, 0.0, [bWST])
        for g in range(8):
            c, hf = g // 2, g % 2
            dma(WST[64 * hf:64 * hf + 64, c, 64 * hf:64 * hf + 64], lru_wa[g], [bWST], [bWST], "wst")
            dma(WST[64 * hf:64 * hf + 64, 4 + c, 64 * hf:64 * hf + 64], lru_wi[g], [bWST], [bWST], "wst")
        P.barrier()
        cp("pool", WBD, WST, [bWST], [bWBD])
        memset("pool", ONESR, 1.0, [bONESR])
        memset("dve", DQ, 0.0, [bWST])
        memset("pool", VA[:, :, :, 64:65], 1.0, [bVAone])

        if DBG <= 1:
            return
        load_wblock(WF, wsrc, 2560, 8, bWF, "wf")
        fbk = bank()
        for i in range(NT):
            for kc in range(8):
                mm(pb[fbk][:, i * 8:(i + 1) * 8], xT[:, kc, i * 128:(i + 1) * 128], WF[:, kc, :], kc == 0, kc == 7,
                   [bWF, bxT[i]], pq[fbk])
        tt("dve", FB, pb[fbk][:, 0:128], par[:, 104:232], ALU.add, pq[fbk] + [b_cst], [bFB])
        act(LL, FB, AF.Exp, [bFB], [bLL], scale=-1.0)
        act(LL, LL, AF.Ln, [bLL, b_cst], [bLL], bias=one_ap, scale=1.0)
        cbk = bank()
        mm(pb[cbk][:, 0:128], tri, LL, True, True, [bLL, b_cst], pq[cbk])
        mm(pb[cbk][:, 128:256], ones, LL, True, True, [bLL, b_cst], pq[cbk])
        memset("dve", PRE[:, 0:8], 0.0, [bPRE])
        for j in range(1, 17):
            tt("dve", PRE[:, 8 * j:8 * j + 8], PRE[:, 8 * (j - 1):8 * j], pb[cbk][:, 128 + 8 * (j - 1):128 + 8 * j],
               ALU.add, [bPRE] + pq[cbk], [bPRE])
        tt("dve", CPOS.rearrange("p a b -> p (a b)"), pb[cbk][:, 0:128], PRE[:, 0:128], ALU.add,
           pq[cbk] + [bPRE], [bCPOS])

        if DBG <= 2:
            return
        for p in range(4):
            c = p
            wu, bwu = wload(128 * c)
            wg, bwg = wload(512 + 128 * c)
            memset("dve", U[:, 0:3], 0.0, [bU])
            for tg in range(4):
                bk = bank()
                for kc in range(8):
                    mm(pb[bk][:, :], wu[:, kc, :], xT[:, kc, tg * 512:(tg + 1) * 512], kc == 0, kc == 7,
                       [bwu] + bxT[tg * 4:tg * 4 + 4], pq[bk])
                cp("act", U[:, 3:515], pb[bk][:, :], pq[bk], [bU])
                bk = bank()
                for kc in range(8):
                    mm(pb[bk][:, :], wg[:, kc, :], xT[:, kc, tg * 512:(tg + 1) * 512], kc == 0, kc == 7,
                       [bwg] + bxT[tg * 4:tg * 4 + 4], pq[bk])
                cp("act", GG, pb[bk][:, :], pq[bk], [bGG])
                ts("dve", CV, U[:, 0:512], par[:, 64 + 4 * c:65 + 4 * c], par[:, 80 + c:81 + c], ALU.mult, ALU.add,
                   [bU, b_cst], [bCV])
                for k in range(1, 4):
                    stt(CV, U[:, k:k + 512], par[:, 64 + 4 * c + k:65 + 4 * c + k], CV, ALU.mult, ALU.add,
                        [bU, bCV, b_cst], [bCV])
                cp("pool", CVB, CV, [bCV], [bCVB])
                if tg < 3:
                    cp("dve", U[:, 0:3], U[:, 512:515], [bU], [bU])
                bka = bank()
                mm(pb[bka][:, :], WBD[:, c, :], CVB, True, True, [bWBD, bCVB], pq[bka])
                bki = bank()
                mm(pb[bki][:, :], WBD[:, 4 + c, :], CVB, True, True, [bWBD, bCVB], pq[bki])
                act(RR, pb[bka][:, :], AF.Sigmoid, pq[bka] + [b_cst], [bRR], bias=par[:, 84 + c:85 + c], scale=1.0)
                act(II, pb[bki][:, :], AF.Sigmoid, pq[bki] + [b_cst], [bII], bias=par[:, 88 + c:89 + c], scale=1.0)
                act(S2, RR, AF.Exp, [bRR, b_cst], [bS2], scale=nsp[:, 4 + c:5 + c])
                act(RR, RR, AF.Exp, [bRR, b_cst], [bRR], scale=nsp[:, c:c + 1])
                act(S2, S2, AF.Sqrt, [bS2, b_cst], [bS2], bias=one_ap, scale=-1.0)
                tt("dve", II, II, CV, ALU.mult, [bII, bCV], [bII])
                tt("dve", II, II, S2, ALU.mult, [bII, bS2], [bII])
                k = tg % 2
                init = 0.0 if tg == 0 else HS[1 - k][:, 511:512]
                P.op("dve", (lambda o_, a_, x_, i_: (lambda e: e.tensor_tensor_scan(
                    out=o_, data0=a_, data1=x_, initial=i_, op0=ALU.mult, op1=ALU.add)))(HS[k], RR, II, init),
                    [bRR, bII, bHS[1 - k]], [bHS[k]])
                act(TT_, GG, AF.Square, [bGG], [bTT])
                ts("dve", TT_, TT_, 0.044715, 1.0, ALU.mult, ALU.add, [bTT], [bTT])
                tt("dve", TT_, TT_, GG, ALU.mult, [bTT, bGG], [bTT])
                act(TT_, TT_, AF.Sigmoid, [bTT], [bTT], scale=1.5957691216057308)
                tt("pool", TT_, TT_, GG, ALU.mult, [bTT, bGG], [bTT])
                tt("dve", YA[:, c, tg * 512:(tg + 1) * 512], HS[k], TT_, ALU.mult, [bHS[k], bTT], [bYA[c][tg]])

            if DBG <= 3:
                return
            wq_, bwq = wload(1024 + 128 * p)
            wk_, bwk = wload(1536 + 128 * p)
            wv_, bwv = wload(2048 + 128 * p)
            for tg in range(4):
                proj_T(QT[:, tg * 512:(tg + 1) * 512], bQT[tg], wq_, bwq, tg, "act")
                proj_T(KT[:, tg * 512:(tg + 1) * 512], bKT[tg], wk_, bwk, tg, "dve")
            for i in range(NT):
                bk = bank()
                for kc in range(8):
                    mm(pb[bk][:, 0:128], xT[:, kc, i * 128:(i + 1) * 128], wv_[:, kc, :], kc == 0, kc == 7,
                       [bwv, bxT[i]], [pq[bk][0]])
                cp("act" if i % 2 else "dve", VA[:, i, :, 0:64],
                   pb[bk][:, 0:128].rearrange("p (a b) -> p a b", a=2), [pq[bk][0]], [bVA[i]])
            if DBG <= 4:
                return
            PRE3 = PRE.rearrange("p (a b) -> p a b", b=8)
            for hh in range(2):
                hd = 2 * p + hh
                r0 = 64 * hh
                for G in range(4):
                    eg = PRE[:, 8 * (4 * G + 4) + hd:8 * (4 * G + 4) + hd + 1]
                    ts("dve", BIASH[hh][:, G, 0:4 * G + 4], CPOS[:, 0:4 * G + 4, hd], eg, None, ALU.subtract, None,
                       [bCPOS, bPRE], [bBIAS[hh]])
                    ts("dve", DIFF[r0:r0 + 1, 16 * hh + 4 * G:16 * hh + 4 * G + 4], PRE3[r0:r0 + 1, 4 * G + 1:4 * G + 5, hd],
                       eg[r0:r0 + 1, :], -8.0, ALU.subtract, ALU.mult, [bPRE], [bDIFF])
                for j in range(NT):
                    ts("dve", DQ[r0:r0 + 1, j * 128:(j + 1) * 128], DQ[r0:r0 + 1, j * 128:(j + 1) * 128], 0.0,
                       DIFF[r0:r0 + 1, 16 * hh + j:16 * hh + j + 1], ALU.mult, ALU.add, [bWST, bDIFF], [bWST])
            items = []
            for hh in range(2):
                for G in range(4):
                    for kb in range(4 * G + 4):
                        items.append(dict(kb=kb, hh=hh, G=G, j0=max(4 * G, kb), j1=4 * G + 4))

            def qk_fn(it):
                r0 = 64 * it["hh"]
                kb, j0, j1 = it["kb"], it["j0"], it["j1"]
                return (KT[r0:r0 + 64, kb * 128:(kb + 1) * 128], QT[r0:r0 + 64, j0 * 128:j1 * 128],
                        [bKT[kb // 4], bQT[j0 // 4]])

            def extra_fn(it):
                r0 = 64 * it["hh"]
                return (ONESR[r0:r0 + 1, :], DQ[r0:r0 + 1, it["j0"] * 128:it["j1"] * 128], [bONESR, bWST])

            def bias_fn(it):
                return BIASH[it["hh"]][:, it["G"], it["kb"]:it["kb"] + 1], [bBIAS[it["hh"]]]

            def mask_fn(it):
                if it["kb"] >= 4 * it["G"]:
                    return [(0, 128, masks[:, 16, :])]
                return []

            def v_fn(it):
                return VA[:, it["kb"], it["hh"], :], [bVA[it["kb"]], bVAone]

            def fin_fn(j, hh, o_ap, o_buf, p=p):
                P.op("dve", lambda e: e.reciprocal(out=OREC[:, hh:hh + 1], in_=o_ap[:, 64:65]), o_buf, [bOREC])
                sl = (2 * j + hh) % 4
                ts("dve", YTOK[sl][:, 0:64], o_ap[:, 0:64], OREC[:, hh:hh + 1], None, ALU.mult, None,
                   o_buf + [bOREC], [bYTOK[sl]])
                tp = pb[7][:, 0:64].bitcast(BF16)[0:64, 0:128]
                P.op("pe", lambda e: e.transpose(out=tp, in_=YTOK[sl][:, 0:64], identity=ident), [bYTOK[sl], b_cst], pq[7])
                cp("act" if j % 2 else "dve", YB[64 * hh:64 * hh + 64, p, j * 128:(j + 1) * 128], tp, pq[7], [bYBh[p][hh][j]])

            attn_wide(items, qk_fn, extra_fn, bias_fn, mask_fn, v_fn, lambda j: 0, fin_fn, PT, bPT, 0.125)
            if DBG <= 5:
                return
        if DBG <= 6:
            return
        P.barrier()
        srcT = [YA[:, c, :] for c in range(4)] + [YB[:, c, :] for c in range(4)]

        def bsrc(kc, i):
            return [bYA[kc][i // 4]] if kc < 4 else [bYBh[kc - 4][0][i], bYBh[kc - 4][1][i]]
        out_proj(srcT, bsrc, s_ab_out, "wo")

    def mixer_l1():
        o = OFF_SCR
        YA = V(o, [128, 4, S], BF16); o += 16384
        YB = V(o, [128, 4, S], BF16); o += 16384
        WB = [V(o + i * 2048, [128, 8, 128], BF16) for i in range(6)]; o += 6 * 2048
        QT = V(o, [128, S], BF16); o += 4096
        KT = V(o, [128, S], BF16); o += 4096
        VA = V(o, [128, NT, 2, 65], BF16); o += 4160
        PT = [V(o + i * 1024, [128, 512], BF16) for i in range(4)]; o += 4096
        YTOK = [V(o + i * 256, [128, 128], BF16) for i in range(4)]; o += 1024
        COS = V(o, [128, S], F32); o += 8192
        SIN = V(o, [128, S], F32); o += 8192
        KD = V(o, [128, S], BF16); o += 4096
        KDS = V(o, [128, S], BF16); o += 4096
        VD = V(o, [128, NT, 2, 65], BF16); o += 4160
        XB = V(o, [128, 512], BF16); o += 1024
        T1 = V(o, [128, 512], F32); o += 2048
        OREC = V(o, [128, 4], F32); o += 16
        assert o <= ARENA_WORDS * 4, o
        bYA = [[[Buf(f"yc{c}_{hf}_{t}") for t in range(NT)] for hf in range(2)] for c in range(4)]
        bYB = [[[Buf(f"yd{c}_{hf}_{t}") for t in range(NT)] for hf in range(2)] for c in range(4)]
        bWB = [Buf(f"wb{i}") for i in range(6)]
        bQT = [Buf(f"qt{t}") for t in range(4)]
        bKT = [Buf(f"kt{t}") for t in range(4)]
        bKD = [Buf(f"kd{t}") for t in range(4)]
        bKDS = [Buf(f"kds{t}") for t in range(4)]
        bVA = [Buf(f"va{t}") for t in range(NT)]
        bVD = [Buf(f"vd{t}") for t in range(NT)]
        bVAone = Buf("vaone")
        bPT = [Buf(f"pt{i}") for i in range(4)]
        bYTOK = [Buf(f"ytok{i}") for i in range(4)]
        bROPE, bXB, bT1, bOREC = Buf("rope"), Buf("xb"), Buf("t1"), Buf("orec")
        wsrc = s_cd_in
        wctr = [0]

        def wload(c0_, ncols=128):
            i = wctr[0] % 6
            wctr[0] += 1
            load_wblock(WB[i][:, :, 0:ncols], wsrc, c0_, ncols, bWB[i], f"wb{i}")
            return WB[i], bWB[i]

        dma(COS, rope_d[:, 0, :], [], [bROPE], "rope")
        dma(SIN, rope_d[:, 1, :], [], [bROPE], "rope")
        P.barrier()
        memset("pool", VA[:, :, :, 64:65], 1.0, [bVAone])
        memset("pool", VD[:, :, :, 64:65], 1.0, [bVAone])

        def proj_rope(dst, bdst, wblk, bw, tg):
            bk = bank()
            for kc in range(8):
                mm(pb[bk][:, :], wblk[:, kc, :], xT[:, kc, tg * 512:(tg + 1) * 512], kc == 0, kc == 7,
                   [bw] + bxT[tg * 4:tg * 4 + 4], pq[bk])
            cp("act", XB, pb[bk][:, :], pq[bk], [bXB])
            bk2 = bank()
            mm(pb[bk2][:, :], pmat, XB, True, True, [bXB, b_cst], pq[bk2])
            sl = slice(tg * 512, (tg + 1) * 512)
            tt("pool", T1, XB, COS[:, sl], ALU.mult, [bXB, bROPE], [bT1])
            tt("dve", pb[bk2][:, :], pb[bk2][:, :], SIN[:, sl], ALU.mult, pq[bk2] + [bROPE], pq[bk2])
            tt("dve", dst[:, sl], pb[bk2][:, :], T1, ALU.add, pq[bk2] + [bT1], [bdst])

        def proj_v(dst, bdst_list, wblk, bw):
            for i in range(NT):
                bk = bank()
                for kc in range(8):
                    mm(pb[bk][:, 0:128], xT[:, kc, i * 128:(i + 1) * 128], wblk[:, kc, :], kc == 0, kc == 7,
                       [bw, bxT[i]], [pq[bk][0]])
                cp("act" if i % 2 else "dve", dst[:, i, :, 0:64],
                   pb[bk][:, 0:128].rearrange("p (a b) -> p a b", a=2), [pq[bk][0]], [bdst_list[i]])

        def make_fin(YDST, bYDST, p, sink_col):
            def fin_fn(j, hh, o_ap, o_buf):
                if sink_col is None:
                    P.op("dve", lambda e: e.reciprocal(out=OREC[:, hh:hh + 1], in_=o_ap[:, 64:65]), o_buf, [bOREC])
                else:
                    sc_ = sink_col(hh)
                    tt("dve", OREC[:, 2 + hh:3 + hh], o_ap[:, 64:65], sinkexp[:, sc_:sc_ + 1], ALU.add,
                       o_buf + [b_cst], [bOREC])
                    P.op("dve", lambda e: e.reciprocal(out=OREC[:, hh:hh + 1], in_=OREC[:, 2 + hh:3 + hh]), [bOREC], [bOREC])
                sl = (2 * j + hh) % 4
                ts("dve", YTOK[sl][:, 0:64], o_ap[:, 0:64], OREC[:, hh:hh + 1], None, ALU.mult, None,
                   o_buf + [bOREC], [bYTOK[sl]])
                tp = pb[7][:, 0:64].bitcast(BF16)[0:64, 0:128]
                P.op("pe", lambda e: e.transpose(out=tp, in_=YTOK[sl][:, 0:64], identity=ident), [bYTOK[sl], b_cst], pq[7])
                cp("act" if j % 2 else "dve", YDST[64 * hh:64 * hh + 64, p, j * 128:(j + 1) * 128], tp, pq[7], [bYDST[p][hh][j]])
            return fin_fn

        for p in range(4):
            wq_, bwq = wload(128 * p)
            wk_, bwk = wload(512 + 128 * p)
            wv_, bwv = wload(1024 + 128 * p)
            for tg in range(4):
                proj_rope(QT, bQT[tg], wq_, bwq, tg)
                proj_rope(KT, bKT[tg], wk_, bwk, tg)
            proj_v(VA, bVA, wv_, bwv)
            items = []
            for hh in range(2):
                for G in range(4):
                    for kb in range(4 * G + 4):
                        items.append(dict(kb=kb, hh=hh, G=G, j0=max(4 * G, kb), j1=4 * G + 4))

            def qk_fn(it):
                r0 = 64 * it["hh"]
                kb, j0, j1 = it["kb"], it["j0"], it["j1"]
                return (KT[r0:r0 + 64, kb * 128:(kb + 1) * 128], QT[r0:r0 + 64, j0 * 128:j1 * 128],
                        [bKT[kb // 4], bQT[j0 // 4]])

            def mask_fn(it):
                kb, j0, j1 = it["kb"], it["j0"], it["j1"]
                return [(0, (j1 - j0) * 128, masks[:, j0 - kb:j1 - kb, :].rearrange("p a b -> p (a b)"))]

            def v_fn(it):
                return VA[:, it["kb"], it["hh"], :], [bVA[it["kb"]], bVAone]

            attn_wide(items, qk_fn, None, lambda it: (None, []), mask_fn, v_fn, lambda j: 0,
                      make_fin(YA, bYA, p, None), PT, bPT, 0.125)

        wk_, bwk = wload(2048)
        wv_, bwv = wload(2176)
        for tg in range(4):
            proj_rope(KD, bKD[tg], wk_, bwk, tg)
            sl = slice(tg * 512, (tg + 1) * 512)
            cp("dve", KDS[0:64, sl], KD[64:128, sl], [bKD[tg]], [bKDS[tg]])
            cp("dve", KDS[64:128, sl], KD[0:64, sl], [bKD[tg]], [bKDS[tg]])
        proj_v(VD, bVD, wv_, bwv)
        for p in range(4):
            wq_, bwq = wload(1536 + 128 * p)
            for tg in range(4):
                proj_rope(QT, bQT[tg], wq_, bwq, tg)
            kvh = p // 2
            items = []
            for hh in range(2):
                for kb in range(NT):
                    items.append(dict(kb=kb, hh=hh, j0=kb, j1=min(kb + 2, NT)))

            def qk_fn2(it, kvh=kvh):
                hh, kb, j0, j1 = it["hh"], it["kb"], it["j0"], it["j1"]
                r0 = 64 * hh
                src_, bsrc_ = (KD, bKD) if hh == kvh else (KDS, bKDS)
                Rq = [bQT[j0 // 4]] + ([bQT[(j1 - 1) // 4]] if (j1 - 1) // 4 != j0 // 4 else [])
                return (src_[r0:r0 + 64, kb * 128:(kb + 1) * 128], QT[r0:r0 + 64, j0 * 128:j1 * 128],
                        [bsrc_[kb // 4]] + Rq)

            def mask_fn2(it):
                n_ = it["j1"] - it["j0"]
                return [(0, n_ * 128, masks[:, 16:16 + n_, :].rearrange("p a b -> p (a b)"))]

            def v_fn2(it, kvh=kvh):
                return VD[:, it["kb"], kvh, :], [bVD[it["kb"]], bVAone]

            attn_wide(items, qk_fn2, None, lambda it: (None, []), mask_fn2, v_fn2, lambda j: max(0, j - 1),
                      make_fin(YB, bYB, p, lambda hh, p=p: 2 * p + hh), PT, bPT, 0.125)
        P.barrier()
        srcT = [YA[:, c, :] for c in range(4)] + [YB[:, c, :] for c in range(4)]

        def bsrc(kc, i):
            return [bYA[kc][0][i], bYA[kc][1][i]] if kc < 4 else [bYB[kc - 4][0][i], bYB[kc - 4][1][i]]
        out_proj(srcT, bsrc, s_cd_out, "wo")

    def xattn(l, b):
        o = OFF_SCR
        o += 8448
        MEM = V(o, [128, 2, D], F32); o += 8192
        MT = V(o, [128, 8, 256], BF16); o += 4096
        KX = V(o, [128, 8, 256], BF16); o += 4096
        VX = V(o, [128, 2, 4, 257], BF16); o += 4112
        QX = V(o, [128, 8, S], BF16); o += 32768
        WK = V(o, [128, 8, 512], BF16); o += 8192
        WQ = [V(o + i * 2048, [128, 8, 128], BF16) for i in range(4)]; o += 8192
        PT = [V(o + i * 1024, [128, 512], BF16) for i in range(4)]; o += 4096
        OTB = [V(o + i * 2048, [128, D], BF16) for i in range(4)]; o += 8192
        OREC = V(o, [128, 4], F32); o += 16
        assert o <= ARENA_WORDS * 4, o
        bMEM = [Buf("mem0"), Buf("mem1")]
        bMT = [Buf("mt0"), Buf("mt1")]
        bKX = [Buf(f"kx{i}") for i in range(8)]
        bVX = [[Buf(f"vx{m}_{c}") for c in range(2)] for m in range(2)]
        bVXone = Buf("vxone")
        bQX = [[Buf(f"qx{f}_{t}") for t in range(4)] for f in range(8)]
        bWK = Buf("wk")
        bWQ = [Buf(f"wq{i}") for i in range(4)]
        bPT = [Buf(f"xpt{i}") for i in range(4)]
        bOTB = [Buf(f"ot{i}") for i in range(4)]
        bOREC = Buf("orec")
        mv = mem_d[b].rearrange("(t p) d -> p t d", p=128)
        dma(MEM, mv, [], bMEM, "mem")
        P.barrier()
        norm_to_T(MEM, bMEM, 2, MT, bMT)
        kvv = s_kv[l].rearrange("(c p) n -> p c n", p=128)
        for half in range(2):
            dma(WK, kvv[:, :, half * 512:(half + 1) * 512], [], [bWK], "wk")
            for f4 in range(4):
                fb = half * 4 + f4
                bk = bank()
                for kc in range(8):
                    mm(pb[bk][:, 0:256], WK[:, kc, f4 * 128:(f4 + 1) * 128], MT[:, kc, :], kc == 0, kc == 7,
                       [bWK] + bMT, pq[bk][0:2])
                cp("act" if fb % 2 else "dve", KX[:, fb, :], pb[bk][:, 0:256], pq[bk][0:2], [bKX[fb]])
        memset("pool", VX[:, :, :, 256:257], 1.0, [bVXone])
        for ch in range(2):
            dma(WK, kvv[:, :, D + ch * 512:D + (ch + 1) * 512], [], [bWK], "wk")
            for mt in range(2):
                bk = bank()
                for kc in range(8):
                    mm(pb[bk][:, :], MT[:, kc, mt * 128:(mt + 1) * 128], WK[:, kc, :],
                       kc == 0, kc == 7, [bWK, bMT[mt]], pq[bk])
                cp("act" if mt else "dve", VX[:, mt, 2 * ch:2 * ch + 2, 0:256],
                   pb[bk][:, :].rearrange("p (a b) -> p a b", a=2), pq[bk], [bVX[mt][ch]])
        for fb in range(8):
            i = fb % 4
            load_wblock(WQ[i], s_q[l], fb * 128, 128, bWQ[i], f"wq{i}")
            for tg in range(4):
                proj_T(QX[:, fb, tg * 512:(tg + 1) * 512], bQX[fb][tg], WQ[i], bWQ[i], tg, "act" if tg % 2 else "dve")
        cnt = 0
        for tg in range(4):
            for hd in range(4):
                pts = []
                for mt in range(2):
                    bk = bank(0, 6)
                    for dh in range(2):
                        f = 2 * hd + dh
                        mm(pb[bk][:, :], KX[:, f, mt * 128:(mt + 1) * 128], QX[:, f, tg * 512:(tg + 1) * 512],
                           dh == 0, dh == 1, [bKX[f], bQX[f][tg]], pq[bk])
                    ps = cnt % 4
                    cnt += 1
                    act(PT[ps], pb[bk][:, :], AF.Exp, pq[bk], [bPT[ps]], scale=1.0 / 16.0)
                    pts.append(ps)
                for qi in range(4):
                    bk = 6 + (qi % 2)
                    for mt in range(2):
                        mm(pb[bk][:, 0:257], PT[pts[mt]][:, qi * 128:(qi + 1) * 128], VX[:, mt, hd, :], mt == 0, mt == 1,
                           [bPT[pts[mt]], bVX[mt][hd // 2], bVXone], pq[bk][0:3])
                    P.op("dve", (lambda o_, i_: (lambda e: e.reciprocal(out=o_, in_=i_)))(OREC[:, qi:qi + 1], pb[bk][:, 256:257]),
                         pq[bk][0:3], [bOREC])
                    ts("dve", OTB[qi][:, hd * 256:(hd + 1) * 256], pb[bk][:, 0:256], OREC[:, qi:qi + 1], None,
                       ALU.mult, None, pq[bk][0:3] + [bOREC], [bOTB[qi]])
            for qi in range(4):
                i = 4 * tg + qi
                bk = bank(0, 6)
                pbh = pb[bk][:].bitcast(BF16)
                for c in range(8):
                    P.op("pe", (lambda o_, i_: (lambda e: e.transpose(out=o_, in_=i_, identity=ident)))(
                        pbh[:, c * 128:(c + 1) * 128], OTB[qi][:, c * 128:(c + 1) * 128]),
                        [bOTB[qi], b_cst], pq[bk])
                cp("act" if qi % 2 else "dve", xT[:, :, i * 128:(i + 1) * 128],
                   pbh.rearrange("p (a b) -> p a b", a=8), pq[bk], [bxT[i]])
        P.barrier()
        out_proj([xT[:, c, :] for c in range(8)], lambda kc, i: [bxT[i]], s_o[l], "wo")

    def mlp(l):
        o = OFF_SCR
        HID = V(o, [128, 32, 512], BF16); o += 32768
        WU = [V(o + i * 8192, [128, 8, 512], BF16) for i in range(2)]; o += 16384
        WD = [V(o + i * 4096, [128, 4, 512], BF16) for i in range(4)]; o += 16384
        RL = [V(o + i * 2048, [128, 512], F32) for i in range(3)]; o += 6144
        assert o <= ARENA_WORDS * 4, o
        bHID = [Buf(f"hid{i}") for i in range(32)]
        bWU = [Buf("wu0"), Buf("wu1")]
        bWD = [Buf(f"wd{i}") for i in range(4)]
        bRL = [Buf(f"rl{i}") for i in range(3)]
        upv = s_up[l].rearrange("(c p) n -> p c n", p=128)
        dnv = s_dn[l].rearrange("(c p) n -> p c n", p=128)
        nu = 0
        nd = 0
        nr = 0
        for tg in range(4):
            for g in range(8):
                wi = nu % 2
                nu += 1
                dma(WU[wi], upv[:, :, g * 512:(g + 1) * 512], [], [bWU[wi]], f"wu{wi}")
                for c4 in range(4):
                    cch = g * 4 + c4
                    bk = bank(0, 4)
                    for kc in range(8):
                        mm(pb[bk][:, :], WU[wi][:, kc, c4 * 128:(c4 + 1) * 128], xT[:, kc, tg * 512:(tg + 1) * 512],
                           kc == 0, kc == 7, [bWU[wi]] + bxT[tg * 4:tg * 4 + 4], pq[bk])
                    ri = nr % 3
                    nr += 1
                    ts("dve", RL[ri], pb[bk][:, :], 0.0, None, ALU.max, None, pq[bk], [bRL[ri]])
                    act(HID[:, cch, :], RL[ri], AF.Square, [bRL[ri]], [bHID[cch]])
            for ch in range(2):
                banks = [4 + qi for qi in range(4)]
                for g in range(8):
                    wi = nd % 4
                    nd += 1
                    dma(WD[wi], dnv[:, g * 4:(g + 1) * 4, ch * 512:(ch + 1) * 512], [], [bWD[wi]], f"wd{wi}")
                    for qi in range(4):
                        for c4 in range(4):
                            cch = g * 4 + c4
                            mm(pb[banks[qi]][:, :], HID[:, cch, qi * 128:(qi + 1) * 128], WD[wi][:, c4, :],
                               cch == 0, cch == 31, [bHID[cch], bWD[wi]], pq[banks[qi]])
                for qi in range(4):
                    i = tg * 4 + qi
                    hs = h[:, i, ch * 512:(ch + 1) * 512]
                    tt("dve", hs, pb[banks[qi]][:, :], hs, ALU.add, pq[banks[qi]] + [bh[i]], [bh[i]])

    def final_out(b):
        o = OFF_SCR
        SS = V(o, [128, 16], F32); o += 64
        STD = V(o, [128, 16], F32); o += 64
        RSTD = V(o, [128, 16], F32); o += 64
        JUNK = [V(o + i * 2048, [128, D], BF16) for i in range(2)]; o += 4096
        OB = [V(o + i * 4096, [128, D], F32) for i in range(4)]; o += 16384
        bSTD, bRSTD = Buf("std"), Buf("rstd")
        bSS = [Buf(f"ss{i}") for i in range(16)]
        bJ = [Buf("junk0"), Buf("junk1")]
        bOB = [Buf(f"ob{i}") for i in range(4)]
        yv = y_d[b].rearrange("(t p) d -> p t d", p=128)
        outs = []
        if raw_out:
            for i in range(NT):
                outs.append(dma(yv[:, i, :], h[:, i, :], [bh[i]], [], f"yo{i % 4}"))
            return outs
        for i in range(NT):
            act(JUNK[i % 2], h[:, i, :], AF.Square, [bh[i]], [bJ[i % 2], bSS[i]], accum=SS[:, i:i + 1])
        act(STD, SS, AF.Sqrt, bSS + [b_cst], [bSTD], bias=eps_ap, scale=1.0 / D)
        P.op("dve", lambda e: e.reciprocal(out=RSTD, in_=STD), [bSTD], [bRSTD])
        for i in range(NT):
            k = i % 4
            stt(OB[k], h[:, i, :], RSTD[:, i:i + 1], gfin, ALU.mult, ALU.mult, [bh[i], bRSTD, b_cst], [bOB[k]])
            outs.append(dma(yv[:, i, :], OB[k], [bOB[k]], [], f"yo{k}"))
        return outs

    setup()
    if stop != "dbg_setup":
        prepass()
    finals = []
    order = ["l0mix", "l0xa", "l0mlp", "l1mix", "l1xa", "l1mlp"]
    if stop in ("dbg_setup", "dbg_pre"):
        nsteps = 0
    elif stop == "dbg_norm":
        nsteps = 0
    else:
        nsteps = len(order) if stop is None else order.index(stop) + 1
    for b in range(nseq):
        xv = x_d[b].rearrange("(t p) d -> p t d", p=128)
        P.barrier()
        for q in range(4):
            dma(h[:, 4 * q:4 * q + 4, :], xv[:, 4 * q:4 * q + 4, :], [], bh[4 * q:4 * q + 4], f"hx{q}")
        for st_i in range(nsteps):
            nm = order[st_i]
            l = 0 if nm.startswith("l0") else 1
            P.barrier()
            norm_to_T(h, bh, NT, xT, bxT)
            P.barrier()
            if nm.endswith("mix"):
                (mixer_l0 if l == 0 else mixer_l1)()
            elif nm.endswith("xa"):
                xattn(l, b)
            else:
                mlp(l)
        if stop == "dbg_norm":
            P.barrier()
            norm_to_T(h, bh, NT, xT, bxT)
        P.barrier()
        finals += final_out(b)
    P.emit(finals)
    es.close()
    return nc


_NC_CACHE = {}


def _get_nc(nseq, stop=None, raw_out=False):
    key = (nseq, stop, raw_out)
    if key not in _NC_CACHE:
        _NC_CACHE[key] = build_program(nseq, stop, raw_out)
    return _NC_CACHE[key]


def make_in_maps(inputs, ncores, nseq):
    cst, rope = _host_consts()
    par = _pack_params(inputs)
    gfin = np.ascontiguousarray(np.broadcast_to(np.asarray(inputs["final_norm"], np.float32)[None, :], (128, D)))
    f = lambda a: np.ascontiguousarray(np.asarray(a, np.float32))
    shared = {
        "ab_w_in": f(inputs["ab_w_in"][0]), "ab_w_out": f(inputs["ab_w_out"][0]),
        "cd_w_in": f(inputs["cd_w_in"][0]), "cd_w_out": f(inputs["cd_w_out"][0]),
        "xa_w_q": f(inputs["xa_w_q"]), "xa_w_kv": f(inputs["xa_w_kv"]), "xa_w_o": f(inputs["xa_w_o"]),
        "mlp_w_up": f(inputs["mlp_w_up"]), "mlp_w_down": f(inputs["mlp_w_down"]),
        "lru_w_a": f(inputs["lru_w_a"][0]), "lru_w_i": f(inputs["lru_w_i"][0]),
        "params": par, "gfin": gfin, "cst": cst, "rope": rope,
    }
    maps = []
    for c in range(ncores):
        m = dict(shared)
        m["x"] = f(inputs["x"][c * nseq:(c + 1) * nseq])
        m["mem"] = f(inputs["mem"][c * nseq:(c + 1) * nseq])
        maps.append(m)
    return maps


def kernel(**inputs):
    nc = _get_nc(SEQ_PER_CORE)
    maps = make_in_maps(inputs, NCORES, SEQ_PER_CORE)
    res = run_bass_kernel_spmd(nc, maps, core_ids=list(range(NCORES)))
    return np.concatenate([np.asarray(r["y"], np.float32) for r in res.results], axis=0)
```

```python
import os
import numpy as np
import concourse.bass as bass
import concourse.mybir as mybir
from concourse.bass_utils import run_bass_kernel_spmd

F32 = mybir.dt.float32
BF16 = mybir.dt.bfloat16
AF = mybir.ActivationFunctionType
ALU = mybir.AluOpType
AX = mybir.AxisListType

ENG_NAMES = ("pe", "act", "dve", "pool", "sp")
SEM_ROT = 2000
DMA_ROT = 120


class Buf:
    __slots__ = ("name", "w", "r", "rd", "const")

    def __init__(self, name):
        self.name = name
        self.w = None
        self.r = {}
        self.rd = []
        self.const = False


class Op:
    __slots__ = ("eng", "fn", "deps", "is_dma", "dkey", "sig", "dval", "id")


class Prog:
    def __init__(self, nc):
        self.nc = nc
        self.ops = []
        self.streams = {e: [] for e in ENG_NAMES}
        self.dma_keys = {}
        self.pending_bar = {e: None for e in ENG_NAMES}
        self.dma_since_bar = []

    def barrier(self):
        deps = set(self.dma_since_bar)
        for e in ENG_NAMES:
            if self.streams[e]:
                deps.add(self.streams[e][-1].id)
            if self.pending_bar[e]:
                deps |= self.pending_bar[e]
        for e in ENG_NAMES:
            self.pending_bar[e] = set(deps)
        self.dma_since_bar = []

    def op(self, eng, fn, R=(), W=(), dma=None):
        o = Op()
        o.eng = eng
        o.fn = fn
        o.is_dma = dma is not None
        o.dkey = dma
        o.sig = None
        o.dval = None
        o.id = len(self.ops)
        deps = set()
        raw = set()
        for b in R:
            if b.w is not None:
                deps.add(b.w)
                raw.add(b.w)
        for b in W:
            if b.w is not None:
                deps.add(b.w)
            deps.update(b.r.values())
            deps.update(b.rd)
        keep = set()
        for d in deps:
            od = self.ops[d]
            if od.eng == eng and not od.is_dma and not o.is_dma:
                if eng != "pe":
                    keep.add(d)
            else:
                keep.add(d)
        if self.pending_bar[eng] is not None:
            for d in self.pending_bar[eng]:
                od = self.ops[d]
                if od.is_dma or od.eng != eng:
                    keep.add(d)
            self.pending_bar[eng] = None
        o.deps = keep
        for b in R:
            if b.const:
                continue
            if o.is_dma:
                b.rd.append(o.id)
            else:
                b.r[eng] = o.id
        for b in W:
            b.w = o.id
            b.r = {}
            b.rd = []
        self.ops.append(o)
        self.streams[eng].append(o)
        if o.is_dma:
            self.dma_keys.setdefault(dma, 0)
            self.dma_since_bar.append(o.id)
        return o

    def emit(self, final_wait_ops=()):
        nc = self.nc
        ops = self.ops
        needed = set()
        for o in ops:
            for d in o.deps:
                needed.add(d)
        for o in final_wait_ops:
            needed.add(o.id)
        ncnt = {e: 0 for e in ENG_NAMES}
        dcnt = {k: 0 for k in self.dma_keys}
        for o in ops:
            if o.is_dma:
                dcnt[o.dkey] += 1
                o.dval = dcnt[o.dkey]
            elif o.id in needed:
                ncnt[o.eng] += 1
                o.sig = ncnt[o.eng]
        from contextlib import ExitStack
        with ExitStack() as es:
            esems = {}
            for e in ENG_NAMES:
                n = max(1, (ncnt[e] + SEM_ROT - 1) // SEM_ROT)
                esems[e] = [es.enter_context(nc.semaphore(f"s_{e}{i}")) for i in range(n)]
            dsems = {k: [es.enter_context(nc.semaphore(f"d_{k}_{i}")) for i in range((dcnt[k] + DMA_ROT - 1) // DMA_ROT)]
                     for k in self.dma_keys}
            self.nsems = sum(len(v) for v in esems.values()) + sum(len(v) for v in dsems.values())
            block = es.enter_context(nc.Block())

            def sem_of(o):
                if o.is_dma:
                    n = o.dval - 1
                    return dsems[o.dkey][n // DMA_ROT], 16 * ((n % DMA_ROT) + 1)
                s = o.sig - 1
                return esems[o.eng][s // SEM_ROT], (s % SEM_ROT) + 1

            def run(ename, engobj):
                waited = {}
                for o in self.streams[ename]:
                    need = {}
                    for d in o.deps:
                        od = ops[d]
                        if od.is_dma:
                            key = ("d", od.dkey)
                            v = od.dval
                        else:
                            key = ("e", od.eng)
                            v = od.sig
                        if waited.get(key, 0) >= v:
                            continue
                        if need.get(key, (0, None))[0] < v:
                            need[key] = (v, od)
                    for key, (v, od) in need.items():
                        s, sv = sem_of(od)
                        engobj.wait_ge(s, sv)
                        waited[key] = v
                    ins = o.fn(engobj)
                    if o.is_dma:
                        ins.then_inc(sem_of(o)[0], 16)
                    elif o.sig is not None:
                        s, sv = sem_of(o)
                        ins.then_inc(s, 1)
                if ename == "sp":
                    last = {}
                    for o in final_wait_ops:
                        last[o.dkey] = o
                    for o in last.values():
                        s, sv = sem_of(o)
                        engobj.wait_ge(s, sv)

            @block.tensor
            def _(e):
                run("pe", e)

            @block.scalar
            def _(e):
                run("act", e)

            @block.vector
            def _(e):
                run("dve", e)

            @block.gpsimd
            def _(e):
                run("pool", e)

            @block.sync
            def _(e):
                run("sp", e)
        return ncnt


S = 2048
D = 1024
NT = 16
NCORES = 8
SEQ_PER_CORE = 4
AB_IN = 2568
CD_IN = 2304
ARENA_WORDS = 51200
OFF_H = 0
OFF_XT = 65536
OFF_CONST = 98304
OFF_SCR = 109568
NCST = 128 * 4 + 18 * 128
NPAR = 232
M_CAUSAL = 0
M_SWA1 = 17


def _host_consts():
    ident = np.eye(128, dtype=np.float32)
    s = np.arange(128)[:, None]
    t = np.arange(128)[None, :]
    tri = (s <= t).astype(np.float32)
    ones = np.ones((128, 128), np.float32)
    pm = np.zeros((128, 128), np.float32)
    for hh in range(2):
        for i in range(32):
            pm[64 * hh + i + 32, 64 * hh + i] = -1.0
            pm[64 * hh + i, 64 * hh + i + 32] = 1.0
    masks = np.zeros((128, 18, 128), np.float32)
    for dlt in range(16):
        dist = 128 * dlt + (t - s)
        m = ((dist >= 0) & (dist <= 128)).astype(np.float32)
        m += ((dist >= 0) & (dist <= 512) & (dist % 4 == 0)).astype(np.float32)
        m += ((dist >= 0) & (dist <= 2048) & (dist % 16 == 0)).astype(np.float32)
        masks[:, dlt, :] = m
    masks[:, 16, :] = (s <= t).astype(np.float32)
    masks[:, 17, :] = (s > t).astype(np.float32)
    cst = np.concatenate([ident, tri, ones, pm, masks.reshape(128, -1)], axis=1).astype(np.float32)
    half = 32
    inv = (10000.0 ** (-np.arange(half, dtype=np.float32) / half)).astype(np.float32)
    pos = np.arange(S, dtype=np.float32)
    ang = pos[None, :] * inv[:, None]
    cos = np.cos(ang).astype(np.float32)
    sin = np.sin(ang).astype(np.float32)
    rope = np.zeros((128, 2, S), np.float32)
    for r in range(128):
        rope[r, 0] = cos[r % 32]
        rope[r, 1] = sin[r % 32]
    return cst, rope


def _pack_params(inp):
    p = np.zeros((128, NPAR), np.float32)

    def col8(v):
        return np.ascontiguousarray(np.asarray(v, np.float32).reshape(8, 128).T)
    p[:, 0:8] = col8(inp["ab_norm"][0])
    p[:, 8:16] = col8(inp["cd_norm"][0])
    p[:, 16:24] = col8(inp["xa_norm"][0])
    p[:, 24:32] = col8(inp["xa_norm"][1])
    p[:, 32:40] = col8(inp["xa_mem_norm"][0])
    p[:, 40:48] = col8(inp["xa_mem_norm"][1])
    p[:, 48:56] = col8(inp["mlp_norm"][0])
    p[:, 56:64] = col8(inp["mlp_norm"][1])
    cw = np.asarray(inp["ab_conv_w"][0], np.float32)
    for c in range(4):
        for k in range(4):
            p[:, 64 + c * 4 + k] = cw[k, c * 128:(c + 1) * 128]
    p[:, 80:84] = np.asarray(inp["ab_conv_b"][0], np.float32).reshape(4, 128).T
    p[:, 84:88] = np.asarray(inp["lru_b_a"][0], np.float32).reshape(4, 128).T
    p[:, 88:92] = np.asarray(inp["lru_b_i"][0], np.float32).reshape(4, 128).T
    p[:, 92:96] = np.asarray(inp["lru_lambda"][0], np.float32).reshape(4, 128).T
    p[:, 96:104] = np.asarray(inp["cd_sink"][0], np.float32)[None, :]
    p[:, 104:232] = np.tile(np.asarray(inp["fox_b_f"][0], np.float32), 16)[None, :]
    return p


def build_program(nseq=SEQ_PER_CORE, stop=None, raw_out=False):
    from contextlib import ExitStack
    nc = bass.Bass("TRN2", target_bir_lowering=False)

    def din(name, shape, dt=F32):
        return nc.dram_tensor(name, list(shape), dt, kind="ExternalInput").ap()

    def dscr(name, shape, dt=BF16):
        return nc.dram_tensor(name, list(shape), dt, kind="Internal").ap()

    x_d = din("x", [nseq, S, D])
    mem_d = din("mem", [nseq, 256, D])
    w_ab_in = din("ab_w_in", [D, AB_IN])
    w_ab_out = din("ab_w_out", [D, D])
    w_cd_in = din("cd_w_in", [D, CD_IN])
    w_cd_out = din("cd_w_out", [D, D])
    w_q = din("xa_w_q", [2, D, D])
    w_kv = din("xa_w_kv", [2, D, 2 * D])
    w_o = din("xa_w_o", [2, D, D])
    w_up = din("mlp_w_up", [2, D, 4 * D])
    w_dn = din("mlp_w_down", [2, 4 * D, D])
    lru_wa = din("lru_w_a", [8, 64, 64])
    lru_wi = din("lru_w_i", [8, 64, 64])
    par_d = din("params", [128, NPAR])
    gfin_d = din("gfin", [128, D])
    cst_d = din("cst", [128, NCST])
    rope_d = din("rope", [128, 2, S])
    y_d = nc.dram_tensor("y", [nseq, S, D], F32, kind="ExternalOutput").ap()

    s_ab_in = dscr("s_ab_in", [D, AB_IN])
    s_ab_out = dscr("s_ab_out", [D, D])
    s_cd_in = dscr("s_cd_in", [D, CD_IN])
    s_cd_out = dscr("s_cd_out", [D, D])
    s_q = [dscr(f"s_q{l}", [D, D]) for l in range(2)]
    s_kv = [dscr(f"s_kv{l}", [D, 2 * D]) for l in range(2)]
    s_o = [dscr(f"s_o{l}", [D, D]) for l in range(2)]
    s_up = [dscr(f"s_up{l}", [D, 4 * D]) for l in range(2)]
    s_dn = [dscr(f"s_dn{l}", [4 * D, D]) for l in range(2)]

    P = Prog(nc)
    es = ExitStack()
    arena = es.enter_context(nc.sbuf_tensor("arena", [128, ARENA_WORDS], F32))
    pb = [es.enter_context(nc.psum_tensor(f"pb{i}", [128, 512], F32)) for i in range(8)]
    pq = [[Buf(f"pb{i}q{j}") for j in range(4)] for i in range(8)]
    bank_ctr = [0]

    def bank(lo=0, hi=8):
        i = lo + bank_ctr[0] % (hi - lo)
        bank_ctr[0] += 1
        return i

    def V(off, shape, dt):
        n = 1
        for s_ in shape[1:]:
            n *= s_
        nbytes = n * (4 if dt == F32 else 2)
        assert off % 4 == 0 and nbytes % 4 == 0, (off, shape)
        assert off + nbytes <= ARENA_WORDS * 4, (off, shape)
        ap = arena[:, off // 4:(off + nbytes) // 4]
        if dt != F32:
            ap = ap.bitcast(dt)
        if len(shape) == 3:
            ap = ap.rearrange("p (a b) -> p a b", a=shape[1])
        elif len(shape) == 4:
            ap = ap.rearrange("p (a b c) -> p a b c", a=shape[1], b=shape[2])
        if shape[0] < 128:
            ap = ap[0:shape[0]]
        return ap

    def mm(out, lhsT, rhs, st, sp, R, W):
        return P.op("pe", lambda e: e.matmul(out, lhsT=lhsT, rhs=rhs, start=st, stop=sp), R, W)

    def act(out, in_, func, R, W, bias=None, scale=None, accum=None):
        kw = {}
        if bias is not None:
            kw["bias"] = bias
        if scale is not None:
            kw["scale"] = scale
        if accum is not None:
            kw["accum_out"] = accum
        return P.op("act", lambda e: e.activation(out=out, in_=in_, func=func, **kw), R, W)

    def cp(eng, out, in_, R, W):
        if eng == "act":
            return P.op("act", lambda e: e.copy(out=out, in_=in_), R, W)
        return P.op(eng, lambda e: e.tensor_copy(out=out, in_=in_), R, W)

    def ts(eng, out, in0, s1, s2, op0, op1, R, W):
        if op1 is None:
            return P.op(eng, lambda e: e.tensor_scalar(out=out, in0=in0, scalar1=s1, scalar2=None, op0=op0), R, W)
        return P.op(eng, lambda e: e.tensor_scalar(out=out, in0=in0, scalar1=s1, scalar2=s2, op0=op0, op1=op1), R, W)

    def tt(eng, out, in0, in1, op, R, W):
        return P.op(eng, lambda e: e.tensor_tensor(out=out, in0=in0, in1=in1, op=op), R, W)

    def stt(out, in0, scalar, in1, op0, op1, R, W):
        return P.op("dve", lambda e: e.scalar_tensor_tensor(out=out, in0=in0, scalar=scalar, in1=in1, op0=op0, op1=op1), R, W)

    def dma(out, in_, R, W, key, eng="sp"):
        return P.op(eng, lambda e: e.dma_start(out=out, in_=in_), R, W, dma=key)

    def memset(eng, ap, val, W):
        return P.op(eng, lambda e: e.memset(ap, val), (), W)

    h = V(OFF_H, [128, NT, D], F32)
    bh = [Buf(f"h{i}") for i in range(NT)]
    xT = V(OFF_XT, [128, 8, S], BF16)
    bxT = [Buf(f"xT{i}") for i in range(NT)]
    c0 = OFF_CONST
    ident = V(c0, [128, 128], BF16); c0 += 256
    masks = V(c0, [128, 18, 128], BF16); c0 += 18 * 256
    tri = V(c0, [128, 128], F32); c0 += 512
    ones = V(c0, [128, 128], F32); c0 += 512
    pmat = V(c0, [128, 128], BF16); c0 += 256
    gfin = V(c0, [128, D], F32); c0 += 4096
    par = V(c0, [128, NPAR], F32); c0 += NPAR * 4
    nsp = V(c0, [128, 8], F32); c0 += 32
    sinkexp = V(c0, [128, 8], F32); c0 += 32
    cone = V(c0, [128, 2], F32); c0 += 8
    assert c0 <= OFF_SCR, c0
    b_cst = Buf("cst")
    one_ap = cone[:, 0:1]
    eps_ap = cone[:, 1:2]

    def setup():
        stg = V(0, [128, NCST], F32)
        b_stg = Buf("stg")
        dma(stg, cst_d, [], [b_stg], "init")
        dma(gfin, gfin_d, [], [b_cst], "init")
        dma(par, par_d, [], [b_cst], "init")
        P.barrier()
        cp("dve", ident, stg[:, 0:128], [b_stg], [b_cst])
        cp("dve", tri, stg[:, 128:256], [b_stg], [b_cst])
        cp("dve", ones, stg[:, 256:384], [b_stg], [b_cst])
        cp("dve", pmat, stg[:, 384:512], [b_stg], [b_cst])
        cp("dve", masks, stg[:, 512:NCST].rearrange("p (a b) -> p a b", a=18), [b_stg], [b_cst])
        memset("dve", cone[:, 0:1], 1.0, [b_cst])
        memset("dve", cone[:, 1:2], 1e-6, [b_cst])
        act(nsp[:, 0:4], par[:, 92:96], AF.Exp, [b_cst], [b_cst], scale=-1.0)
        act(nsp[:, 0:4], nsp[:, 0:4], AF.Ln, [b_cst], [b_cst], bias=one_ap, scale=1.0)
        ts("dve", nsp[:, 4:8], nsp[:, 0:4], -16.0, None, ALU.mult, None, [b_cst], [b_cst])
        ts("dve", nsp[:, 0:4], nsp[:, 0:4], -8.0, None, ALU.mult, None, [b_cst], [b_cst])
        act(sinkexp, par[:, 96:104], AF.Exp, [b_cst], [b_cst])
        P.barrier()
        b_cst.const = True

    def prepass():
        specs = [
            (w_ab_in, 0, s_ab_in), (w_ab_out, None, s_ab_out),
            (w_cd_in, 8, s_cd_in), (w_cd_out, None, s_cd_out),
        ]
        for l in range(2):
            specs += [(w_q[l], 16 + 8 * l, s_q[l]), (w_kv[l], 32 + 8 * l, s_kv[l]), (w_o[l], None, s_o[l]),
                      (w_up[l], 48 + 8 * l, s_up[l]), (w_dn[l], None, s_dn[l])]
        NS = 6
        st = [V(i * 8192, [128, 2048], F32) for i in range(NS)]
        ob = [V(NS * 8192 + i * 4096, [128, 2048], BF16) for i in range(NS)]
        b_st = [Buf(f"pst{i}") for i in range(NS)]
        b_ob = [Buf(f"pob{i}") for i in range(NS)]
        b_dst = Buf("wscr")
        k = 0
        for (src, gcol, dst) in specs:
            R_, C_ = src.shape
            for rt in range(R_ // 128):
                for cc in range(0, C_, 2048):
                    cw = min(2048, C_ - cc)
                    sl = k % NS
                    dma(st[sl][:, :cw], src[rt * 128:(rt + 1) * 128, cc:cc + cw], [], [b_st[sl]], f"pst{sl}")
                    eng = ("dve", "act")[k % 2]
                    if gcol is None:
                        cp(eng, ob[sl][:, :cw], st[sl][:, :cw], [b_st[sl]], [b_ob[sl]])
                    else:
                        g = par[:, gcol + rt:gcol + rt + 1]
                        if eng == "act":
                            act(ob[sl][:, :cw], st[sl][:, :cw], AF.Copy, [b_st[sl]], [b_ob[sl]], scale=g)
                        else:
                            ts(eng, ob[sl][:, :cw], st[sl][:, :cw], g, None, ALU.mult, None, [b_st[sl]], [b_ob[sl]])
                    dma(dst[rt * 128:(rt + 1) * 128, cc:cc + cw], ob[sl][:, :cw], [b_ob[sl]], [b_dst], f"pob{sl}", eng="act")
                    k += 1
        P.barrier()

    def norm_to_T(src3, bsrc, ntiles, dstT, bdst):
        o = OFF_SCR
        SS = V(o, [128, 16], F32); o += 64
        STD = V(o, [128, 16], F32); o += 64
        RSTD = V(o, [128, 16], F32); o += 64
        JUNK = [V(o + i * 2048, [128, D], BF16) for i in range(2)]; o += 4096
        XS = [V(o + i * 2048, [128, D], BF16) for i in range(2)]
        bSTD, bRSTD = Buf("std"), Buf("rstd")
        bSS = [Buf(f"ss{i}") for i in range(16)]
        bJ = [Buf("junk0"), Buf("junk1")]
        bXS = [Buf("xs0"), Buf("xs1")]
        for i in range(ntiles):
            act(JUNK[i % 2], src3[:, i, :], AF.Square, [bsrc[i]], [bJ[i % 2], bSS[i]], accum=SS[:, i:i + 1])
        act(STD[:, :ntiles], SS[:, :ntiles], AF.Sqrt, bSS[:ntiles] + [b_cst], [bSTD], bias=eps_ap, scale=1.0 / D)
        P.op("dve", lambda e: e.reciprocal(out=RSTD[:, :ntiles], in_=STD[:, :ntiles]), [bSTD], [bRSTD])
        for i in range(ntiles):
            k = i % 2
            ts("dve", XS[k], src3[:, i, :], RSTD[:, i:i + 1], None, ALU.mult, None, [bsrc[i], bRSTD], [bXS[k]])
            bk = bank()
            pbh = pb[bk][:].bitcast(BF16)
            for c in range(8):
                P.op("pe", (lambda o_, i_: (lambda e: e.transpose(out=o_, in_=i_, identity=ident)))(
                    pbh[:, c * 128:(c + 1) * 128], XS[k][:, c * 128:(c + 1) * 128]), [bXS[k], b_cst], pq[bk])
            cp("act" if i % 2 == 0 else "dve", dstT[:, :, i * 128:(i + 1) * 128],
               pbh.rearrange("p (a b) -> p a b", a=8), pq[bk], [bdst[i]])

    def load_wblock(dst_ap, src_scr, c0_, ncols, bdst, key, nk=8):
        v = src_scr.rearrange("(c p) n -> p c n", p=128)
        return dma(dst_ap, v[:, 0:nk, c0_:c0_ + ncols], [], [bdst], key)

    def proj_T(dst_ap, bdst, wblk, bw, tg, evac_eng, rows=128):
        bk = bank()
        for kc in range(8):
            mm(pb[bk][0:rows, :], wblk[:, kc, 0:rows], xT[:, kc, tg * 512:(tg + 1) * 512], kc == 0, kc == 7,
               [bw] + bxT[tg * 4:tg * 4 + 4], pq[bk])
        cp(evac_eng, dst_ap, pb[bk][0:rows, :], pq[bk], [bdst])
        return bk

    def out_proj(srcT_list, bsrc_fn, w_scr, wkey):
        o = OFF_SCR + 32768
        WO = V(o, [128, 8, D], BF16)
        bWO = [Buf("wo0"), Buf("wo1")]
        wv = w_scr.rearrange("(c p) n -> p c n", p=128)
        for ch in range(2):
            dma(WO[:, :, ch * 512:(ch + 1) * 512], wv[:, :, ch * 512:(ch + 1) * 512], [], [bWO[ch]], f"{wkey}{ch}")
        for i in range(NT):
            for ch in range(2):
                bk = bank()
                for kc in range(8):
                    mm(pb[bk][:, :], srcT_list[kc][:, i * 128:(i + 1) * 128], WO[:, kc, ch * 512:(ch + 1) * 512],
                       kc == 0, kc == 7, [bWO[ch]] + bsrc_fn(kc, i), pq[bk])
                hs = h[:, i, ch * 512:(ch + 1) * 512]
                tt("dve", hs, pb[bk][:, :], hs, ALU.add, pq[bk] + [bh[i]], [bh[i]])

    def attn_pair(items, qk_fn, bias_fn, mask_fn, v_fn, fin_fn, PT, bPT, scale):
        LA = 3
        n = len(items)
        state = {}
        grp_bank = {}
        gcount = [0]

        def qk(idx):
            it = items[idx]
            bk = idx % 5
            sc = pb[bk][:, 0:128]
            lhsT, rhs, Rb = qk_fn(it)
            mm(sc, lhsT, rhs, True, True, Rb, pq[bk])
            ps = idx % len(PT)
            b_ap, Rbias = bias_fn(it)
            if b_ap is None:
                act(PT[ps], sc, AF.Exp, pq[bk], [bPT[ps]], scale=scale)
            else:
                act(PT[ps], sc, AF.Exp, pq[bk] + Rbias, [bPT[ps]], bias=b_ap, scale=scale)
            m_ap = mask_fn(it)
            if m_ap is not None:
                eng = "dve" if (idx % 2 == 0) else "pool"
                tt(eng, PT[ps], PT[ps], m_ap, ALU.mult, [bPT[ps], b_cst], [bPT[ps]])

        def pv(idx):
            it = items[idx]
            grp, first, last = it[0], it[1], it[2]
            if first:
                grp_bank[grp] = 5 + (gcount[0] % 2)
                gcount[0] += 1
            bk = grp_bank[grp]
            ps = idx % len(PT)
            v_ap, Rv = v_fn(it)
            mm(pb[bk][:, 0:65], PT[ps], v_ap, first, last, [bPT[ps]] + Rv, pq[bk])
            if last:
                fin_fn(it, pb[bk][:, 0:65], pq[bk])
                del grp_bank[grp]

        for idx in range(n + LA):
            if idx < n:
                qk(idx)
            if idx - LA >= 0:
                pv(idx - LA)

    def attn_wide(items, qk_fn, extra_fn, bias_fn, mask_fn, v_fn, first_kb_fn, fin_fn, PT, bPT, scale):
        LA = 2
        n = len(items)

        def qk(idx):
            it = items[idx]
            w = (it["j1"] - it["j0"]) * 128
            bk = idx % 3
            sc = pb[bk][:, 0:w]
            lhsT, rhs, Rb = qk_fn(it)
            ex = extra_fn(it) if extra_fn is not None else None
            mm(sc, lhsT, rhs, True, ex is None, Rb, pq[bk])
            if ex is not None:
                mm(sc, ex[0], ex[1], False, True, ex[2], pq[bk])
            ps = idx % len(PT)
            b_ap, Rbias = bias_fn(it)
            if b_ap is None:
                act(PT[ps][:, 0:w], sc, AF.Exp, pq[bk], [bPT[ps]], scale=scale)
            else:
                act(PT[ps][:, 0:w], sc, AF.Exp, pq[bk] + Rbias, [bPT[ps]], bias=b_ap, scale=scale)
            for (c0_, cw_, m_ap) in mask_fn(it):
                tt("dve", PT[ps][:, c0_:c0_ + cw_], PT[ps][:, c0_:c0_ + cw_], m_ap, ALU.mult, [bPT[ps], b_cst], [bPT[ps]])

        def pv(idx):
            it = items[idx]
            kb, hh = it["kb"], it["hh"]
            ps = idx % len(PT)
            v_ap, Rv = v_fn(it)
            for j in range(it["j0"], it["j1"]):
                bk = 3 + (j % 4)
                c = (j - it["j0"]) * 128
                first = kb == first_kb_fn(j)
                last = kb == j
                mm(pb[bk][:, 0:65], PT[ps][:, c:c + 128], v_ap, first, last, [bPT[ps]] + Rv, pq[bk])
                if last:
                    fin_fn(j, hh, pb[bk][:, 0:65], pq[bk])

        for idx in range(n + LA):
            if idx < n:
                qk(idx)
            if idx - LA >= 0:
                pv(idx - LA)

    def mixer_l0():
        DBG = int(os.environ.get("DBG_L0", "99"))
        o = OFF_SCR
        YA = V(o, [128, 4, S], BF16); o += 16384
        YB = V(o, [128, 4, S], BF16); o += 16384
        WB = [V(o + i * 2048, [128, 8, 128], BF16) for i in range(6)]; o += 6 * 2048
        QT = V(o, [128, S], BF16); o += 4096
        KT = V(o, [128, S], BF16); o += 4096
        VA = V(o, [128, NT, 2, 65], BF16); o += 4160
        PT = [V(o + i * 1024, [128, 512], BF16) for i in range(4)]; o += 4096
        YTOK = [V(o + i * 256, [128, 128], BF16) for i in range(4)]; o += 1024
        BIASH = [V(o + i * 256, [128, 4, 16], F32) for i in range(2)]; o += 2048
        ONESR = V(o, [128, 128], BF16); o += 256
        DIFF = V(o, [128, 32], F32); o += 128
        FB = V(o, [128, 128], F32); o += 512
        LL = V(o, [128, 128], F32); o += 512
        PRE = V(o, [128, 136], F32); o += 544
        CPOS = V(o, [128, 16, 8], F32); o += 512
        WF = V(o, [128, 8, 8], BF16); o += 128
        WBD = V(o, [128, 8, 128], BF16); o += 2048
        U = V(o, [128, 516], F32); o += 2064
        CV = V(o, [128, 512], F32); o += 2048
        CVB = V(o, [128, 512], BF16); o += 1024
        RR = V(o, [128, 512], F32); o += 2048
        II = V(o, [128, 512], F32); o += 2048
        S2 = V(o, [128, 512], F32); o += 2048
        GG = V(o, [128, 512], F32); o += 2048
        TT_ = V(o, [128, 512], F32); o += 2048
        HS = [V(o + i * 2048, [128, 512], F32) for i in range(2)]; o += 4096
        WST = V(o, [128, 8, 128], F32)
        DQ = V(o, [128, S], BF16); o += 4096
        OREC = V(o, [128, 4], F32); o += 16
        assert o <= ARENA_WORDS * 4, o
        bYA = [[Buf(f"ya{c}_{t}") for t in range(4)] for c in range(4)]
        bYBh = [[[Buf(f"yb{c}_{hf}_{t}") for t in range(NT)] for hf in range(2)] for c in range(4)]
        bWB = [Buf(f"wb{i}") for i in range(6)]
        bQT = [Buf(f"qt{t}") for t in range(4)]
        bKT = [Buf(f"kt{t}") for t in range(4)]
        bVA = [Buf(f"va{t}") for t in range(NT)]
        bVAone = Buf("vaone")
        bPT = [Buf(f"pt{i}") for i in range(4)]
        bYTOK = [Buf(f"ytok{i}") for i in range(4)]
        bBIAS = [Buf("biash0"), Buf("biash1")]
        bONESR, bDIFF = Buf("onesr"), Buf("diff")
        bFB, bLL, bPRE, bCPOS, bWF, bWBD, bWST = (Buf(n_) for n_ in ("fb", "ll", "pre", "cpos", "wf", "wbd", "wst"))
        bU, bCV, bCVB, bRR, bII, bS2, bGG, bTT = (Buf(n_) for n_ in ("u", "cv", "cvb", "rr", "ii", "s2", "gg", "tt"))
        bHS = [Buf("hs0"), Buf("hs1")]
        bOREC = Buf("orec")
        wsrc = s_ab_in
        wctr = [0]

        def wload(c0_, ncols=128):
            i = wctr[0] % 6
            wctr[0] += 1
            load_wblock(WB[i][:, :, 0:ncols], wsrc, c0_, ncols, bWB[i], f"wb{i}")
            return WB[i], bWB[i]

        memset("pool", WST, 0.0, [bWST])
        for g in range(8):
            c, hf = g // 2, g % 2
            dma(WST[64 * hf:64 * hf + 64, c, 64 * hf:64 * hf + 64], lru_wa[g], [bWST], [bWST], "wst")
            dma(WST[64 * hf:64 * hf + 64, 4 + c, 64 * hf:64 * hf + 64], lru_wi[g], [bWST], [bWST], "wst")
        P.barrier()
        cp("pool", WBD, WST, [bWST], [bWBD])
        memset("pool", ONESR, 1.0, [bONESR])
        memset("dve", DQ, 0.0, [bWST])
        memset("pool", VA[:, :, :, 64:65], 1.0, [bVAone])

        if DBG <= 1:
            return
        load_wblock(WF, wsrc, 2560, 8, bWF, "wf")
        fbk = bank()
        for i in range(NT):
            for kc in range(8):
                mm(pb[fbk][:, i * 8:(i + 1) * 8], xT[:, kc, i * 128:(i + 1) * 128], WF[:, kc, :], kc == 0, kc == 7,
                   [bWF, bxT[i]], pq[fbk])
        tt("dve", FB, pb[fbk][:, 0:128], par[:, 104:232], ALU.add, pq[fbk] + [b_cst], [bFB])
        act(LL, FB, AF.Exp, [bFB], [bLL], scale=-1.0)
        act(LL, LL, AF.Ln, [bLL, b_cst], [bLL], bias=one_ap, scale=1.0)
        cbk = bank()
        mm(pb[cbk][:, 0:128], tri, LL, True, True, [bLL, b_cst], pq[cbk])
        mm(pb[cbk][:, 128:256], ones, LL, True, True, [bLL, b_cst], pq[cbk])
        memset("dve", PRE[:, 0:8], 0.0, [bPRE])
        for j in range(1, 17):
            tt("dve", PRE[:, 8 * j:8 * j + 8], PRE[:, 8 * (j - 1):8 * j], pb[cbk][:, 128 + 8 * (j - 1):128 + 8 * j],
               ALU.add, [bPRE] + pq[cbk], [bPRE])
        tt("dve", CPOS.rearrange("p a b -> p (a b)"), pb[cbk][:, 0:128], PRE[:, 0:128], ALU.add,
           pq[cbk] + [bPRE], [bCPOS])

        if DBG <= 2:
            return
        for p in range(4):
            c = p
            wu, bwu = wload(128 * c)
            wg, bwg = wload(512 + 128 * c)
            memset("dve", U[:, 0:3], 0.0, [bU])
            for tg in range(4):
                bk = bank()
                for kc in range(8):
                    mm(pb[bk][:, :], wu[:, kc, :], xT[:, kc, tg * 512:(tg + 1) * 512], kc == 0, kc == 7,
                       [bwu] + bxT[tg * 4:tg * 4 + 4], pq[bk])
                cp("act", U[:, 3:515], pb[bk][:, :], pq[bk], [bU])
                bk = bank()
                for kc in range(8):
                    mm(pb[bk][:, :], wg[:, kc, :], xT[:, kc, tg * 512:(tg + 1) * 512], kc == 0, kc == 7,
                       [bwg] + bxT[tg * 4:tg * 4 + 4], pq[bk])
                cp("act", GG, pb[bk][:, :], pq[bk], [bGG])
                ts("dve", CV, U[:, 0:512], par[:, 64 + 4 * c:65 + 4 * c], par[:, 80 + c:81 + c], ALU.mult, ALU.add,
                   [bU, b_cst], [bCV])
                for k in range(1, 4):
                    stt(CV, U[:, k:k + 512], par[:, 64 + 4 * c + k:65 + 4 * c + k], CV, ALU.mult, ALU.add,
                        [bU, bCV, b_cst], [bCV])
                cp("pool", CVB, CV, [bCV], [bCVB])
                if tg < 3:
                    cp("dve", U[:, 0:3], U[:, 512:515], [bU], [bU])
                bka = bank()
                mm(pb[bka][:, :], WBD[:, c, :], CVB, True, True, [bWBD, bCVB], pq[bka])
                bki = bank()
                mm(pb[bki][:, :], WBD[:, 4 + c, :], CVB, True, True, [bWBD, bCVB], pq[bki])
                act(RR, pb[bka][:, :], AF.Sigmoid, pq[bka] + [b_cst], [bRR], bias=par[:, 84 + c:85 + c], scale=1.0)
                act(II, pb[bki][:, :], AF.Sigmoid, pq[bki] + [b_cst], [bII], bias=par[:, 88 + c:89 + c], scale=1.0)
                act(S2, RR, AF.Exp, [bRR, b_cst], [bS2], scale=nsp[:, 4 + c:5 + c])
                act(RR, RR, AF.Exp, [bRR, b_cst], [bRR], scale=nsp[:, c:c + 1])
                act(S2, S2, AF.Sqrt, [bS2, b_cst], [bS2], bias=one_ap, scale=-1.0)
                tt("dve", II, II, CV, ALU.mult, [bII, bCV], [bII])
                tt("dve", II, II, S2, ALU.mult, [bII, bS2], [bII])
                k = tg % 2
                init = 0.0 if tg == 0 else HS[1 - k][:, 511:512]
                P.op("dve", (lambda o_, a_, x_, i_: (lambda e: e.tensor_tensor_scan(
                    out=o_, data0=a_, data1=x_, initial=i_, op0=ALU.mult, op1=ALU.add)))(HS[k], RR, II, init),
                    [bRR, bII, bHS[1 - k]], [bHS[k]])
                act(TT_, GG, AF.Square, [bGG], [bTT])
                ts("dve", TT_, TT_, 0.044715, 1.0, ALU.mult, ALU.add, [bTT], [bTT])
                tt("dve", TT_, TT_, GG, ALU.mult, [bTT, bGG], [bTT])
                act(TT_, TT_, AF.Sigmoid, [bTT], [bTT], scale=1.5957691216057308)
                tt("pool", TT_, TT_, GG, ALU.mult, [bTT, bGG], [bTT])
                tt("dve", YA[:, c, tg * 512:(tg + 1) * 512], HS[k], TT_, ALU.mult, [bHS[k], bTT], [bYA[c][tg]])

            if DBG <= 3:
                return
            wq_, bwq = wload(1024 + 128 * p)
            wk_, bwk = wload(1536 + 128 * p)
            wv_, bwv = wload(2048 + 128 * p)
            for tg in range(4):
                proj_T(QT[:, tg * 512:(tg + 1) * 512], bQT[tg], wq_, bwq, tg, "act")
                proj_T(KT[:, tg * 512:(tg + 1) * 512], bKT[tg], wk_, bwk, tg, "dve")
            for i in range(NT):
                bk = bank()
                for kc in range(8):
                    mm(pb[bk][:, 0:128], xT[:, kc, i * 128:(i + 1) * 128], wv_[:, kc, :], kc == 0, kc == 7,
                       [bwv, bxT[i]], [pq[bk][0]])
                cp("act" if i % 2 else "dve", VA[:, i, :, 0:64],
                   pb[bk][:, 0:128].rearrange("p (a b) -> p a b", a=2), [pq[bk][0]], [bVA[i]])
            if DBG <= 4:
                return
            PRE3 = PRE.rearrange("p (a b) -> p a b", b=8)
            for hh in range(2):
                hd = 2 * p + hh
                r0 = 64 * hh
                for G in range(4):
                    eg = PRE[:, 8 * (4 * G + 4) + hd:8 * (4 * G + 4) + hd + 1]
                    ts("dve", BIASH[hh][:, G, 0:4 * G + 4], CPOS[:, 0:4 * G + 4, hd], eg, None, ALU.subtract, None,
                       [bCPOS, bPRE], [bBIAS[hh]])
                    ts("dve", DIFF[r0:r0 + 1, 16 * hh + 4 * G:16 * hh + 4 * G + 4], PRE3[r0:r0 + 1, 4 * G + 1:4 * G + 5, hd],
                       eg[r0:r0 + 1, :], -8.0, ALU.subtract, ALU.mult, [bPRE], [bDIFF])
                for j in range(NT):
                    ts("dve", DQ[r0:r0 + 1, j * 128:(j + 1) * 128], DQ[r0:r0 + 1, j * 128:(j + 1) * 128], 0.0,
                       DIFF[r0:r0 + 1, 16 * hh + j:16 * hh + j + 1], ALU.mult, ALU.add, [bWST, bDIFF], [bWST])
            items = []
            for hh in range(2):
                for G in range(4):
                    for kb in range(4 * G + 4):
                        items.append(dict(kb=kb, hh=hh, G=G, j0=max(4 * G, kb), j1=4 * G + 4))

            def qk_fn(it):
                r0 = 64 * it["hh"]
                kb, j0, j1 = it["kb"], it["j0"], it["j1"]
                return (KT[r0:r0 + 64, kb * 128:(kb + 1) * 128], QT[r0:r0 + 64, j0 * 128:j1 * 128],
                        [bKT[kb // 4], bQT[j0 // 4]])

            def extra_fn(it):
                r0 = 64 * it["hh"]
                return (ONESR[r0:r0 + 1, :], DQ[r0:r0 + 1, it["j0"] * 128:it["j1"] * 128], [bONESR, bWST])

            def bias_fn(it):
                return BIASH[it["hh"]][:, it["G"], it["kb"]:it["kb"] + 1], [bBIAS[it["hh"]]]

            def mask_fn(it):
                if it["kb"] >= 4 * it["G"]:
                    return [(0, 128, masks[:, 16, :])]
                return []

            def v_fn(it):
                return VA[:, it["kb"], it["hh"], :], [bVA[it["kb"]], bVAone]

            def fin_fn(j, hh, o_ap, o_buf, p=p):
                P.op("dve", lambda e: e.reciprocal(out=OREC[:, hh:hh + 1], in_=o_ap[:, 64:65]), o_buf, [bOREC])
                sl = (2 * j + hh) % 4
                ts("dve", YTOK[sl][:, 0:64], o_ap[:, 0:64], OREC[:, hh:hh + 1], None, ALU.mult, None,
                   o_buf + [bOREC], [bYTOK[sl]])
                tp = pb[7][:, 0:64].bitcast(BF16)[0:64, 0:128]
                P.op("pe", lambda e: e.transpose(out=tp, in_=YTOK[sl][:, 0:64], identity=ident), [bYTOK[sl], b_cst], pq[7])
                cp("act" if j % 2 else "dve", YB[64 * hh:64 * hh + 64, p, j * 128:(j + 1) * 128], tp, pq[7], [bYBh[p][hh][j]])

            attn_wide(items, qk_fn, extra_fn, bias_fn, mask_fn, v_fn, lambda j: 0, fin_fn, PT, bPT, 0.125)
            if DBG <= 5:
                return
        if DBG <= 6:
            return
        P.barrier()
        srcT = [YA[:, c, :] for c in range(4)] + [YB[:, c, :] for c in range(4)]

        def bsrc(kc, i):
            return [bYA[kc][i // 4]] if kc < 4 else [bYBh[kc - 4][0][i], bYBh[kc - 4][1][i]]
        out_proj(srcT, bsrc, s_ab_out, "wo")

    def mixer_l1():
        o = OFF_SCR
        YA = V(o, [128, 4, S], BF16); o += 16384
        YB = V(o, [128, 4, S], BF16); o += 16384
        WB = [V(o + i * 2048, [128, 8, 128], BF16) for i in range(6)]; o += 6 * 2048
        QT = V(o, [128, S], BF16); o += 4096
        KT = V(o, [128, S], BF16); o += 4096
        VA = V(o, [128, NT, 2, 65], BF16); o += 4160
        PT = [V(o + i * 1024, [128, 512], BF16) for i in range(4)]; o += 4096
        YTOK = [V(o + i * 256, [128, 128], BF16) for i in range(4)]; o += 1024
        COS = V(o, [128, S], F32); o += 8192
        SIN = V(o, [128, S], F32); o += 8192
        KD = V(o, [128, S], BF16); o += 4096
        KDS = V(o, [128, S], BF16); o += 4096
        VD = V(o, [128, NT, 2, 65], BF16); o += 4160
        XB = V(o, [128, 512], BF16); o += 1024
        T1 = V(o, [128, 512], F32); o += 2048
        OREC = V(o, [128, 4], F32); o += 16
        assert o <= ARENA_WORDS * 4, o
        bYA = [[[Buf(f"yc{c}_{hf}_{t}") for t in range(NT)] for hf in range(2)] for c in range(4)]
        bYB = [[[Buf(f"yd{c}_{hf}_{t}") for t in range(NT)] for hf in range(2)] for c in range(4)]
        bWB = [Buf(f"wb{i}") for i in range(6)]
        bQT = [Buf(f"qt{t}") for t in range(4)]
        bKT = [Buf(f"kt{t}") for t in range(4)]
        bKD = [Buf(f"kd{t}") for t in range(4)]
        bKDS = [Buf(f"kds{t}") for t in range(4)]
        bVA = [Buf(f"va{t}") for t in range(NT)]
        bVD = [Buf(f"vd{t}") for t in range(NT)]
        bVAone = Buf("vaone")
        bPT = [Buf(f"pt{i}") for i in range(4)]
        bYTOK = [Buf(f"ytok{i}") for i in range(4)]
        bROPE, bXB, bT1, bOREC = Buf("rope"), Buf("xb"), Buf("t1"), Buf("orec")
        wsrc = s_cd_in
        wctr = [0]

        def wload(c0_, ncols=128):
            i = wctr[0] % 6
            wctr[0] += 1
            load_wblock(WB[i][:, :, 0:ncols], wsrc, c0_, ncols, bWB[i], f"wb{i}")
            return WB[i], bWB[i]

        dma(COS, rope_d[:, 0, :], [], [bROPE], "rope")
        dma(SIN, rope_d[:, 1, :], [], [bROPE], "rope")
        P.barrier()
        memset("pool", VA[:, :, :, 64:65], 1.0, [bVAone])
        memset("pool", VD[:, :, :, 64:65], 1.0, [bVAone])

        def proj_rope(dst, bdst, wblk, bw, tg):
            bk = bank()
            for kc in range(8):
                mm(pb[bk][:, :], wblk[:, kc, :], xT[:, kc, tg * 512:(tg + 1) * 512], kc == 0, kc == 7,
                   [bw] + bxT[tg * 4:tg * 4 + 4], pq[bk])
            cp("act", XB, pb[bk][:, :], pq[bk], [bXB])
            bk2 = bank()
            mm(pb[bk2][:, :], pmat, XB, True, True, [bXB, b_cst], pq[bk2])
            sl = slice(tg * 512, (tg + 1) * 512)
            tt("pool", T1, XB, COS[:, sl], ALU.mult, [bXB, bROPE], [bT1])
            tt("dve", pb[bk2][:, :], pb[bk2][:, :], SIN[:, sl], ALU.mult, pq[bk2] + [bROPE], pq[bk2])
            tt("dve", dst[:, sl], pb[bk2][:, :], T1, ALU.add, pq[bk2] + [bT1], [bdst])

        def proj_v(dst, bdst_list, wblk, bw):
            for i in range(NT):
                bk = bank()
                for kc in range(8):
                    mm(pb[bk][:, 0:128], xT[:, kc, i * 128:(i + 1) * 128], wblk[:, kc, :], kc == 0, kc == 7,
                       [bw, bxT[i]], [pq[bk][0]])
                cp("act" if i % 2 else "dve", dst[:, i, :, 0:64],
                   pb[bk][:, 0:128].rearrange("p (a b) -> p a b", a=2), [pq[bk][0]], [bdst_list[i]])

        def make_fin(YDST, bYDST, p, sink_col):
            def fin_fn(j, hh, o_ap, o_buf):
                if sink_col is None:
                    P.op("dve", lambda e: e.reciprocal(out=OREC[:, hh:hh + 1], in_=o_ap[:, 64:65]), o_buf, [bOREC])
                else:
                    sc_ = sink_col(hh)
                    tt("dve", OREC[:, 2 + hh:3 + hh], o_ap[:, 64:65], sinkexp[:, sc_:sc_ + 1], ALU.add,
                       o_buf + [b_cst], [bOREC])
                    P.op("dve", lambda e: e.reciprocal(out=OREC[:, hh:hh + 1], in_=OREC[:, 2 + hh:3 + hh]), [bOREC], [bOREC])
                sl = (2 * j + hh) % 4
                ts("dve", YTOK[sl][:, 0:64], o_ap[:, 0:64], OREC[:, hh:hh + 1], None, ALU.mult, None,
                   o_buf + [bOREC], [bYTOK[sl]])
                tp = pb[7][:, 0:64].bitcast(BF16)[0:64, 0:128]
                P.op("pe", lambda e: e.transpose(out=tp, in_=YTOK[sl][:, 0:64], identity=ident), [bYTOK[sl], b_cst], pq[7])
                cp("act" if j % 2 else "dve", YDST[64 * hh:64 * hh + 64, p, j * 128:(j + 1) * 128], tp, pq[7], [bYDST[p][hh][j]])
            return fin_fn

        for p in range(4):
            wq_, bwq = wload(128 * p)
            wk_, bwk = wload(512 + 128 * p)
            wv_, bwv = wload(1024 + 128 * p)
            for tg in range(4):
                proj_rope(QT, bQT[tg], wq_, bwq, tg)
                proj_rope(KT, bKT[tg], wk_, bwk, tg)
            proj_v(VA, bVA, wv_, bwv)
            items = []
            for hh in range(2):
                for G in range(4):
                    for kb in range(4 * G + 4):
                        items.append(dict(kb=kb, hh=hh, G=G, j0=max(4 * G, kb), j1=4 * G + 4))

            def qk_fn(it):
                r0 = 64 * it["hh"]
                kb, j0, j1 = it["kb"], it["j0"], it["j1"]
                return (KT[r0:r0 + 64, kb * 128:(kb + 1) * 128], QT[r0:r0 + 64, j0 * 128:j1 * 128],
                        [bKT[kb // 4], bQT[j0 // 4]])

            def mask_fn(it):
                kb, j0, j1 = it["kb"], it["j0"], it["j1"]
                return [(0, (j1 - j0) * 128, masks[:, j0 - kb:j1 - kb, :].rearrange("p a b -> p (a b)"))]

            def v_fn(it):
                return VA[:, it["kb"], it["hh"], :], [bVA[it["kb"]], bVAone]

            attn_wide(items, qk_fn, None, lambda it: (None, []), mask_fn, v_fn, lambda j: 0,
                      make_fin(YA, bYA, p, None), PT, bPT, 0.125)

        wk_, bwk = wload(2048)
        wv_, bwv = wload(2176)
        for tg in range(4):
            proj_rope(KD, bKD[tg], wk_, bwk, tg)
            sl = slice(tg * 512, (tg + 1) * 512)
            cp("dve", KDS[0:64, sl], KD[64:128, sl], [bKD[tg]], [bKDS[tg]])
            cp("dve", KDS[64:128, sl], KD[0:64, sl], [bKD[tg]], [bKDS[tg]])
        proj_v(VD, bVD, wv_, bwv)
        for p in range(4):
            wq_, bwq = wload(1536 + 128 * p)
            for tg in range(4):
                proj_rope(QT, bQT[tg], wq_, bwq, tg)
            kvh = p // 2
            items = []
            for hh in range(2):
                for kb in range(NT):
                    items.append(dict(kb=kb, hh=hh, j0=kb, j1=min(kb + 2, NT)))

            def qk_fn2(it, kvh=kvh):
                hh, kb, j0, j1 = it["hh"], it["kb"], it["j0"], it["j1"]
                r0 = 64 * hh
                src_, bsrc_ = (KD, bKD) if hh == kvh else (KDS, bKDS)
                Rq = [bQT[j0 // 4]] + ([bQT[(j1 - 1) // 4]] if (j1 - 1) // 4 != j0 // 4 else [])
                return (src_[r0:r0 + 64, kb * 128:(kb + 1) * 128], QT[r0:r0 + 64, j0 * 128:j1 * 128],
                        [bsrc_[kb // 4]] + Rq)

            def mask_fn2(it):
                n_ = it["j1"] - it["j0"]
                return [(0, n_ * 128, masks[:, 16:16 + n_, :].rearrange("p a b -> p (a b)"))]

            def v_fn2(it, kvh=kvh):
                return VD[:, it["kb"], kvh, :], [bVD[it["kb"]], bVAone]

            attn_wide(items, qk_fn2, None, lambda it: (None, []), mask_fn2, v_fn2, lambda j: max(0, j - 1),
                      make_fin(YB, bYB, p, lambda hh, p=p: 2 * p + hh), PT, bPT, 0.125)
        P.barrier()
        srcT = [YA[:, c, :] for c in range(4)] + [YB[:, c, :] for c in range(4)]

        def bsrc(kc, i):
            return [bYA[kc][0][i], bYA[kc][1][i]] if kc < 4 else [bYB[kc - 4][0][i], bYB[kc - 4][1][i]]
        out_proj(srcT, bsrc, s_cd_out, "wo")

    def xattn(l, b):
        o = OFF_SCR
        o += 8448
        MEM = V(o, [128, 2, D], F32); o += 8192
        MT = V(o, [128, 8, 256], BF16); o += 4096
        KX = V(o, [128, 8, 256], BF16); o += 4096
        VX = V(o, [128, 2, 4, 257], BF16); o += 4112
        QX = V(o, [128, 8, S], BF16); o += 32768
        WK = V(o, [128, 8, 512], BF16); o += 8192
        WQ = [V(o + i * 2048, [128, 8, 128], BF16) for i in range(4)]; o += 8192
        PT = [V(o + i * 1024, [128, 512], BF16) for i in range(4)]; o += 4096
        OTB = [V(o + i * 2048, [128, D], BF16) for i in range(4)]; o += 8192
        OREC = V(o, [128, 4], F32); o += 16
        assert o <= ARENA_WORDS * 4, o
        bMEM = [Buf("mem0"), Buf("mem1")]
        bMT = [Buf("mt0"), Buf("mt1")]
        bKX = [Buf(f"kx{i}") for i in range(8)]
        bVX = [[Buf(f"vx{m}_{c}") for c in range(2)] for m in range(2)]
        bVXone = Buf("vxone")
        bQX = [[Buf(f"qx{f}_{t}") for t in range(4)] for f in range(8)]
        bWK = Buf("wk")
        bWQ = [Buf(f"wq{i}") for i in range(4)]
        bPT = [Buf(f"xpt{i}") for i in range(4)]
        bOTB = [Buf(f"ot{i}") for i in range(4)]
        bOREC = Buf("orec")
        mv = mem_d[b].rearrange("(t p) d -> p t d", p=128)
        dma(MEM, mv, [], bMEM, "mem")
        P.barrier()
        norm_to_T(MEM, bMEM, 2, MT, bMT)
        kvv = s_kv[l].rearrange("(c p) n -> p c n", p=128)
        for half in range(2):
            dma(WK, kvv[:, :, half * 512:(half + 1) * 512], [], [bWK], "wk")
            for f4 in range(4):
                fb = half * 4 + f4
                bk = bank()
                for kc in range(8):
                    mm(pb[bk][:, 0:256], WK[:, kc, f4 * 128:(f4 + 1) * 128], MT[:, kc, :], kc == 0, kc == 7,
                       [bWK] + bMT, pq[bk][0:2])
                cp("act" if fb % 2 else "dve", KX[:, fb, :], pb[bk][:, 0:256], pq[bk][0:2], [bKX[fb]])
        memset("pool", VX[:, :, :, 256:257], 1.0, [bVXone])
        for ch in range(2):
            dma(WK, kvv[:, :, D + ch * 512:D + (ch + 1) * 512], [], [bWK], "wk")
            for mt in range(2):
                bk = bank()
                for kc in range(8):
                    mm(pb[bk][:, :], MT[:, kc, mt * 128:(mt + 1) * 128], WK[:, kc, :],
                       kc == 0, kc == 7, [bWK, bMT[mt]], pq[bk])
                cp("act" if mt else "dve", VX[:, mt, 2 * ch:2 * ch + 2, 0:256],
                   pb[bk][:, :].rearrange("p (a b) -> p a b", a=2), pq[bk], [bVX[mt][ch]])
        for fb in range(8):
            i = fb % 4
            load_wblock(WQ[i], s_q[l], fb * 128, 128, bWQ[i], f"wq{i}")
            for tg in range(4):
                proj_T(QX[:, fb, tg * 512:(tg + 1) * 512], bQX[fb][tg], WQ[i], bWQ[i], tg, "act" if tg % 2 else "dve")
        cnt = 0
        for tg in range(4):
            for hd in range(4):
                pts = []
                for mt in range(2):
                    bk = bank(0, 6)
                    for dh in range(2):
                        f = 2 * hd + dh
                        mm(pb[bk][:, :], KX[:, f, mt * 128:(mt + 1) * 128], QX[:, f, tg * 512:(tg + 1) * 512],
                           dh == 0, dh == 1, [bKX[f], bQX[f][tg]], pq[bk])
                    ps = cnt % 4
                    cnt += 1
                    act(PT[ps], pb[bk][:, :], AF.Exp, pq[bk], [bPT[ps]], scale=1.0 / 16.0)
                    pts.append(ps)
                for qi in range(4):
                    bk = 6 + (qi % 2)
                    for mt in range(2):
                        mm(pb[bk][:, 0:257], PT[pts[mt]][:, qi * 128:(qi + 1) * 128], VX[:, mt, hd, :], mt == 0, mt == 1,
                           [bPT[pts[mt]], bVX[mt][hd // 2], bVXone], pq[bk][0:3])
                    P.op("dve", (lambda o_, i_: (lambda e: e.reciprocal(out=o_, in_=i_)))(OREC[:, qi:qi + 1], pb[bk][:, 256:257]),
                         pq[bk][0:3], [bOREC])
                    ts("dve", OTB[qi][:, hd * 256:(hd + 1) * 256], pb[bk][:, 0:256], OREC[:, qi:qi + 1], None,
                       ALU.mult, None, pq[bk][0:3] + [bOREC], [bOTB[qi]])
            for qi in range(4):
                i = 4 * tg + qi
                bk = bank(0, 6)
                pbh = pb[bk][:].bitcast(BF16)
                for c in range(8):
                    P.op("pe", (lambda o_, i_: (lambda e: e.transpose(out=o_, in_=i_, identity=ident)))(
                        pbh[:, c * 128:(c + 1) * 128], OTB[qi][:, c * 128:(c + 1) * 128]),
                        [bOTB[qi], b_cst], pq[bk])
                cp("act" if qi % 2 else "dve", xT[:, :, i * 128:(i + 1) * 128],
                   pbh.rearrange("p (a b) -> p a b", a=8), pq[bk], [bxT[i]])
        P.barrier()
        out_proj([xT[:, c, :] for c in range(8)], lambda kc, i: [bxT[i]], s_o[l], "wo")

    def mlp(l):
        o = OFF_SCR
        HID = V(o, [128, 32, 512], BF16); o += 32768
        WU = [V(o + i * 8192, [128, 8, 512], BF16) for i in range(2)]; o += 16384
        WD = [V(o + i * 4096, [128, 4, 512], BF16) for i in range(4)]; o += 16384
        RL = [V(o + i * 2048, [128, 512], F32) for i in range(3)]; o += 6144
        assert o <= ARENA_WORDS * 4, o
        bHID = [Buf(f"hid{i}") for i in range(32)]
        bWU = [Buf("wu0"), Buf("wu1")]
        bWD = [Buf(f"wd{i}") for i in range(4)]
        bRL = [Buf(f"rl{i}") for i in range(3)]
        upv = s_up[l].rearrange("(c p) n -> p c n", p=128)
        dnv = s_dn[l].rearrange("(c p) n -> p c n", p=128)
        nu = 0
        nd = 0
        nr = 0
        for tg in range(4):
            for g in range(8):
                wi = nu % 2
                nu += 1
                dma(WU[wi], upv[:, :, g * 512:(g + 1) * 512], [], [bWU[wi]], f"wu{wi}")
                for c4 in range(4):
                    cch = g * 4 + c4
                    bk = bank(0, 4)
                    for kc in range(8):
                        mm(pb[bk][:, :], WU[wi][:, kc, c4 * 128:(c4 + 1) * 128], xT[:, kc, tg * 512:(tg + 1) * 512],
                           kc == 0, kc == 7, [bWU[wi]] + bxT[tg * 4:tg * 4 + 4], pq[bk])
                    ri = nr % 3
                    nr += 1
                    ts("dve", RL[ri], pb[bk][:, :], 0.0, None, ALU.max, None, pq[bk], [bRL[ri]])
                    act(HID[:, cch, :], RL[ri], AF.Square, [bRL[ri]], [bHID[cch]])
            for ch in range(2):
                banks = [4 + qi for qi in range(4)]
                for g in range(8):
                    wi = nd % 4
                    nd += 1
                    dma(WD[wi], dnv[:, g * 4:(g + 1) * 4, ch * 512:(ch + 1) * 512], [], [bWD[wi]], f"wd{wi}")
                    for qi in range(4):
                        for c4 in range(4):
                            cch = g * 4 + c4
                            mm(pb[banks[qi]][:, :], HID[:, cch, qi * 128:(qi + 1) * 128], WD[wi][:, c4, :],
                               cch == 0, cch == 31, [bHID[cch], bWD[wi]], pq[banks[qi]])
                for qi in range(4):
                    i = tg * 4 + qi
                    hs = h[:, i, ch * 512:(ch + 1) * 512]
                    tt("dve", hs, pb[banks[qi]][:, :], hs, ALU.add, pq[banks[qi]] + [bh[i]], [bh[i]])

    def final_out(b):
        o = OFF_SCR
        SS = V(o, [128, 16], F32); o += 64
        STD = V(o, [128, 16], F32); o += 64
        RSTD = V(o, [128, 16], F32); o += 64
        JUNK = [V(o + i * 2048, [128, D], BF16) for i in range(2)]; o += 4096
        OB = [V(o + i * 4096, [128, D], F32) for i in range(4)]; o += 16384
        bSTD, bRSTD = Buf("std"), Buf("rstd")
        bSS = [Buf(f"ss{i}") for i in range(16)]
        bJ = [Buf("junk0"), Buf("junk1")]
        bOB = [Buf(f"ob{i}") for i in range(4)]
        yv = y_d[b].rearrange("(t p) d -> p t d", p=128)
        outs = []
        if raw_out:
            for i in range(NT):
                outs.append(dma(yv[:, i, :], h[:, i, :], [bh[i]], [], f"yo{i % 4}"))
            return outs
        for i in range(NT):
            act(JUNK[i % 2], h[:, i, :], AF.Square, [bh[i]], [bJ[i % 2], bSS[i]], accum=SS[:, i:i + 1])
        act(STD, SS, AF.Sqrt, bSS + [b_cst], [bSTD], bias=eps_ap, scale=1.0 / D)
        P.op("dve", lambda e: e.reciprocal(out=RSTD, in_=STD), [bSTD], [bRSTD])
        for i in range(NT):
            k = i % 4
            stt(OB[k], h[:, i, :], RSTD[:, i:i + 1], gfin, ALU.mult, ALU.mult, [bh[i], bRSTD, b_cst], [bOB[k]])
            outs.append(dma(yv[:, i, :], OB[k], [bOB[k]], [], f"yo{k}"))
        return outs

    setup()
    if stop != "dbg_setup":
        prepass()
    finals = []
    order = ["l0mix", "l0xa", "l0mlp", "l1mix", "l1xa", "l1mlp"]
    if stop in ("dbg_setup", "dbg_pre"):
        nsteps = 0
    elif stop == "dbg_norm":
        nsteps = 0
    else:
        nsteps = len(order) if stop is None else order.index(stop) + 1
    for b in range(nseq):
        xv = x_d[b].rearrange("(t p) d -> p t d", p=128)
        P.barrier()
        for q in range(4):
            dma(h[:, 4 * q:4 * q + 4, :], xv[:, 4 * q:4 * q + 4, :], [], bh[4 * q:4 * q + 4], f"hx{q}")
        for st_i in range(nsteps):
            nm = order[st_i]
            l = 0 if nm.startswith("l0") else 1
            P.barrier()
            norm_to_T(h, bh, NT, xT, bxT)
            P.barrier()
            if nm.endswith("mix"):
                (mixer_l0 if l == 0 else mixer_l1)()
            elif nm.endswith("xa"):
                xattn(l, b)
            else:
                mlp(l)
        if stop == "dbg_norm":
            P.barrier()
            norm_to_T(h, bh, NT, xT, bxT)
        P.barrier()
        finals += final_out(b)
    P.emit(finals)
    es.close()
    return nc


_NC_CACHE = {}


def _get_nc(nseq, stop=None, raw_out=False):
    key = (nseq, stop, raw_out)
    if key not in _NC_CACHE:
        _NC_CACHE[key] = build_program(nseq, stop, raw_out)
    return _NC_CACHE[key]


def make_in_maps(inputs, ncores, nseq):
    cst, rope = _host_consts()
    par = _pack_params(inputs)
    gfin = np.ascontiguousarray(np.broadcast_to(np.asarray(inputs["final_norm"], np.float32)[None, :], (128, D)))
    f = lambda a: np.ascontiguousarray(np.asarray(a, np.float32))
    shared = {
        "ab_w_in": f(inputs["ab_w_in"][0]), "ab_w_out": f(inputs["ab_w_out"][0]),
        "cd_w_in": f(inputs["cd_w_in"][0]), "cd_w_out": f(inputs["cd_w_out"][0]),
        "xa_w_q": f(inputs["xa_w_q"]), "xa_w_kv": f(inputs["xa_w_kv"]), "xa_w_o": f(inputs["xa_w_o"]),
        "mlp_w_up": f(inputs["mlp_w_up"]), "mlp_w_down": f(inputs["mlp_w_down"]),
        "lru_w_a": f(inputs["lru_w_a"][0]), "lru_w_i": f(inputs["lru_w_i"][0]),
        "params": par, "gfin": gfin, "cst": cst, "rope": rope,
    }
    maps = []
    for c in range(ncores):
        m = dict(shared)
        m["x"] = f(inputs["x"][c * nseq:(c + 1) * nseq])
        m["mem"] = f(inputs["mem"][c * nseq:(c + 1) * nseq])
        maps.append(m)
    return maps


def kernel(**inputs):
    nc = _get_nc(SEQ_PER_CORE)
    maps = make_in_maps(inputs, NCORES, SEQ_PER_CORE)
    res = run_bass_kernel_spmd(nc, maps, core_ids=list(range(NCORES)))
    return np.concatenate([np.asarray(r["y"], np.float32) for r in res.results], axis=0)
```
